# Optimizing a Trainium2 kernel written in Bass

```python
import math
import jax, jax.numpy as jnp
from jax import lax
import numpy as np

D_MODEL = 2048
BATCH = 1
SEQ = 8192
DEPTH = 4

HEAD_DIM = 128
MLA_HEADS = 8
MLA_Q_RANK = 512
MLA_KV_RANK = 512
MLA_NOPE_DIM = 128
MLA_ROPE_DIM = 64
MLA_V_DIM = 128
MLA_QK_DIM = MLA_NOPE_DIM + MLA_ROPE_DIM
DIL_HEADS = 8
DIL_PATTERNS = ((128, 1), (512, 4), (2048, 16))
ROPE_THETA = 500000.0
PARTIAL_ROPE_DIM = HEAD_DIM // 4
D_FF = 5632
Q_BLOCK = 128
RMS_EPS = 1e-6
NEG = -1e30
MLA_WIDTH = MLA_HEADS * MLA_V_DIM
DIL_WIDTH = DIL_HEADS * HEAD_DIM
D_MIX = MLA_WIDTH + DIL_WIDTH
IN_SIZES = (MLA_Q_RANK, MLA_KV_RANK, MLA_ROPE_DIM, DIL_WIDTH, DIL_WIDTH, DIL_WIDTH)
IN_COLS = sum(IN_SIZES)
IN_SPLITS = tuple(int(v) for v in np.cumsum(IN_SIZES)[:-1])

kernel_name = "hymba_mla_dilated_macaron_sandwich"


def rms_norm(x, g):
    xf = x.astype(jnp.float32)
    y = xf * lax.rsqrt(jnp.mean(xf * xf, axis=-1, keepdims=True) + RMS_EPS)
    return (y * g.astype(jnp.float32)).astype(x.dtype)


def swiglu(h, w_gate, w_up, w_down):
    return (jax.nn.silu(h @ w_gate) * (h @ w_up)) @ w_down


def rope_tables(positions, dim):
    inv = ROPE_THETA ** (-jnp.arange(0, dim, 2, dtype=jnp.float32) / dim)
    ang = positions.astype(jnp.float32)[..., None] * inv
    return jnp.cos(ang), jnp.sin(ang)


def apply_rope(x, cos, sin):
    xf = x.astype(jnp.float32)
    x1, x2 = jnp.split(xf, 2, axis=-1)
    out = jnp.concatenate([x1 * cos - x2 * sin, x2 * cos + x1 * sin], axis=-1)
    return out.astype(x.dtype)


def mla_attention(q_nope, q_rope, k_nope, k_rope, v):
    B, S, H, _ = q_nope.shape
    nb = S // Q_BLOCK
    scale = 1.0 / math.sqrt(MLA_QK_DIM)
    key_idx = jnp.arange(S)

    def to_blocks(t):
        return jnp.moveaxis(t.reshape(B, nb, Q_BLOCK, *t.shape[2:]), 1, 0)

    def one_block(args):
        qn_b, qr_b, i = args
        s = (jnp.einsum('bqhd,bkhd->bhqk', qn_b, k_nope).astype(jnp.float32)
             + jnp.einsum('bqhr,bkr->bhqk', qr_b, k_rope).astype(jnp.float32)) * scale
        q_idx = i * Q_BLOCK + jnp.arange(Q_BLOCK)
        mask = key_idx[None, :] <= q_idx[:, None]
        s = jnp.where(mask, s, NEG)
        p = jax.nn.softmax(s, axis=-1).astype(v.dtype)
        return jnp.einsum('bhqk,bkhd->bqhd', p, v)

    o = lax.map(one_block, (to_blocks(q_nope), to_blocks(q_rope), jnp.arange(nb)))
    return jnp.moveaxis(o, 0, 1).reshape(B, S, H, -1)


def dilated_window_attention(q, k, v, window, dilation):
    B, S, H, Dh = q.shape
    span = window // dilation
    blk = span
    seg = blk * dilation
    s_pad = -(-S // seg) * seg
    L = s_pad // dilation
    nb = L // blk
    scale = 1.0 / math.sqrt(Dh)

    def to_sub(t):
        t = jnp.pad(t, ((0, 0), (0, s_pad - S), (0, 0), (0, 0)))
        t = t.reshape(B, L, dilation, H, Dh).transpose(0, 2, 1, 3, 4)
        return t.reshape(B, dilation, nb, blk, H, Dh)

    def with_prev(t):
        prev = jnp.pad(t[:, :, :-1], ((0, 0), (0, 0), (1, 0), (0, 0), (0, 0), (0, 0)))
        return jnp.concatenate([prev, t], axis=3)

    qb = to_sub(q)
    kc = with_prev(to_sub(k))
    vc = with_prev(to_sub(v))
    s = jnp.einsum('brnqhd,brnkhd->brnhqk', qb, kc).astype(jnp.float32) * scale
    qi = jnp.arange(blk)[:, None]
    kj = jnp.arange(2 * blk)[None, :]
    dist = blk + qi - kj
    band = (dist >= 0) & (dist <= span)
    valid_prev = (jnp.arange(nb)[:, None, None] > 0) | (kj >= blk)[None]
    mask = band[None] & valid_prev
    s = jnp.where(mask[None, None, :, None], s, NEG)
    lse = jax.nn.logsumexp(s, axis=-1)
    p = jnp.exp(s - lse[..., None]).astype(v.dtype)
    o = jnp.einsum('brnhqk,brnkhd->brnqhd', p, vc)

    def from_sub(t):
        rest = t.shape[4:]
        t = t.reshape(B, dilation, L, *rest)
        t = jnp.moveaxis(t, 1, 2).reshape(B, s_pad, *rest)
        return t[:, :S]

    return from_sub(o), from_sub(jnp.moveaxis(lse, -1, -2))


def dilated_mixture(q, k, v):
    outs, lses = [], []
    for window, dilation in DIL_PATTERNS:
        o_p, lse_p = dilated_window_attention(q, k, v, window, dilation)
        outs.append(o_p)
        lses.append(lse_p)
    w = jax.nn.softmax(jnp.stack(lses, axis=0), axis=0)
    return jnp.einsum('pbsh,pbshd->bshd', w.astype(q.dtype), jnp.stack(outs, axis=0))


def setup_inputs(seed: int = 0) -> dict:
    key = jax.random.key(seed)
    ks = jax.random.split(key, 24)

    def w(k, shape, fan_in):
        return jax.random.normal(k, shape, jnp.float32) * (fan_in ** -0.5)

    def gain(k, n):
        return 1.0 + 0.1 * jax.random.normal(k, (DEPTH, n), jnp.float32)

    x = jax.random.normal(ks[0], (BATCH, SEQ, D_MODEL), jnp.float32)
    offset = jax.random.randint(ks[1], (BATCH, 1), 0, 1024, dtype=jnp.int32)
    positions = (offset + jnp.arange(SEQ, dtype=jnp.int32)[None, :]).astype(jnp.int32)
    return {
        "x": x,
        "positions": positions,
        "ffn1_pre_g": gain(ks[2], D_MODEL),
        "ffn1_post_g": gain(ks[3], D_MODEL),
        "ffn1_w_gate": w(ks[4], (DEPTH, D_MODEL, D_FF), D_MODEL),
        "ffn1_w_up": w(ks[5], (DEPTH, D_MODEL, D_FF), D_MODEL),
        "ffn1_w_down": w(ks[6], (DEPTH, D_FF, D_MODEL), D_FF),
        "mix_pre_g": gain(ks[7], D_MODEL),
        "mix_post_g": gain(ks[8], D_MODEL),
        "w_in": w(ks[9], (DEPTH, D_MODEL, IN_COLS), D_MODEL),
        "mla_q_norm_g": gain(ks[10], MLA_Q_RANK),
        "mla_w_uq": w(ks[11], (DEPTH, MLA_Q_RANK, MLA_HEADS * MLA_QK_DIM), MLA_Q_RANK),
        "mla_kv_norm_g": gain(ks[12], MLA_KV_RANK),
        "mla_w_ukv": w(ks[13], (DEPTH, MLA_KV_RANK, MLA_HEADS * (MLA_NOPE_DIM + MLA_V_DIM)), MLA_KV_RANK),
        "w_o": w(ks[14], (DEPTH, D_MIX, D_MODEL), D_MIX),
        "ffn2_pre_g": gain(ks[15], D_MODEL),
        "ffn2_post_g": gain(ks[16], D_MODEL),
        "ffn2_w_gate": w(ks[17], (DEPTH, D_MODEL, D_FF), D_MODEL),
        "ffn2_w_up": w(ks[18], (DEPTH, D_MODEL, D_FF), D_MODEL),
        "ffn2_w_down": w(ks[19], (DEPTH, D_FF, D_MODEL), D_FF),
    }


def reference(x, positions, ffn1_pre_g, ffn1_post_g, ffn1_w_gate, ffn1_w_up, ffn1_w_down,
              mix_pre_g, mix_post_g, w_in, mla_q_norm_g, mla_w_uq, mla_kv_norm_g, mla_w_ukv,
              w_o, ffn2_pre_g, ffn2_post_g, ffn2_w_gate, ffn2_w_up, ffn2_w_down):
    B, S, _ = x.shape
    cos_a, sin_a = rope_tables(positions, MLA_ROPE_DIM)
    cos_p, sin_p = rope_tables(positions, PARTIAL_ROPE_DIM)

    def partial_rope(t):
        return jnp.concatenate([apply_rope(t[..., :PARTIAL_ROPE_DIM], cos_p[:, :, None], sin_p[:, :, None]),
                                t[..., PARTIAL_ROPE_DIM:]], axis=-1)

    for l in range(DEPTH):
        h = rms_norm(x, ffn1_pre_g[l])
        x = x + 0.5 * rms_norm(swiglu(h, ffn1_w_gate[l], ffn1_w_up[l], ffn1_w_down[l]), ffn1_post_g[l])

        h = rms_norm(x, mix_pre_g[l])
        proj = h @ w_in[l]
        c_q, c_kv, k_rope, q_d, k_d, v_d = jnp.split(proj, IN_SPLITS, axis=-1)

        q_a = (rms_norm(c_q, mla_q_norm_g[l]) @ mla_w_uq[l]).reshape(B, S, MLA_HEADS, MLA_QK_DIM)
        q_nope, q_rope = q_a[..., :MLA_NOPE_DIM], q_a[..., MLA_NOPE_DIM:]
        q_rope = apply_rope(q_rope, cos_a[:, :, None], sin_a[:, :, None])
        k_rope = apply_rope(k_rope, cos_a, sin_a)
        kv = (rms_norm(c_kv, mla_kv_norm_g[l]) @ mla_w_ukv[l]).reshape(B, S, MLA_HEADS, MLA_NOPE_DIM + MLA_V_DIM)
        k_nope, v_a = kv[..., :MLA_NOPE_DIM], kv[..., MLA_NOPE_DIM:]
        o_a = mla_attention(q_nope, q_rope, k_nope, k_rope, v_a)

        q_b = partial_rope(q_d.reshape(B, S, DIL_HEADS, HEAD_DIM))
        k_b = partial_rope(k_d.reshape(B, S, DIL_HEADS, HEAD_DIM))
        v_b = v_d.reshape(B, S, DIL_HEADS, HEAD_DIM)
        o_b = dilated_mixture(q_b, k_b, v_b)

        o = jnp.concatenate([o_a.reshape(B, S, MLA_WIDTH), o_b.reshape(B, S, DIL_WIDTH)], axis=-1) @ w_o[l]
        x = x + rms_norm(o, mix_post_g[l])

        h = rms_norm(x, ffn2_pre_g[l])
        x = x + 0.5 * rms_norm(swiglu(h, ffn2_w_gate[l], ffn2_w_up[l], ffn2_w_down[l]), ffn2_post_g[l])
    return x
```

```python
import numpy as np
import concourse.bass as bass
import concourse.mybir as mybir
from concourse.bass_utils import run_bass_kernel_spmd
from contextlib import ExitStack

F32 = mybir.dt.float32
BF16 = mybir.dt.bfloat16
I32 = mybir.dt.int32
AF = mybir.ActivationFunctionType
ALU = mybir.AluOpType

D = 2048
S = 8192
DEPTH = 4
DFF = 5632
NCORE = 8
TOK = 1024
KC = D // 128
FC = DFF // 128
EPS = 1e-6


DBG = {}


class Res:
    __slots__ = ("name", "last_w", "readers", "excl")

    def __init__(self, name, excl=False):
        self.name = name
        self.last_w = None
        self.readers = {}
        self.excl = excl


class Op:
    __slots__ = ("eng", "fn", "dma", "deps", "idx", "sig", "inc", "waits", "users")

    def __init__(self, eng, fn, dma):
        self.eng = eng
        self.fn = fn
        self.dma = dma
        self.deps = set()
        self.sig = None
        self.inc = False
        self.waits = []
        self.users = 0


class Prog:
    NDMASEM = 6

    def __init__(self, nc):
        self.nc = nc
        self.ops = []
        self.final = []

    def add(self, eng, fn, r=(), w=(), dma=False, final=False):
        o = Op(eng, fn, dma)
        o.idx = len(self.ops)
        deps = set()
        for res in r:
            d = res.last_w
            if d is not None:
                if d.dma or dma or d.eng != eng or eng != "pe":
                    deps.add(d)
            if res.excl:
                for d in res.readers.values():
                    if d.eng != eng:
                        deps.add(d)
        for res in w:
            d = res.last_w
            if d is not None and (d.dma or dma or d.eng != eng or eng != "pe"):
                deps.add(d)
            for d in res.readers.values():
                if d is not o and (d.dma or dma or d.eng != eng or eng != "pe"):
                    deps.add(d)
        for res in r:
            key = ("dma", o.idx) if dma else eng
            res.readers[key] = o
        for res in w:
            res.last_w = o
            res.readers = {}
        o.deps = deps
        for d in deps:
            d.users += 1
        self.ops.append(o)
        if final:
            self.final.append(o)
        return o

    def emit(self):
        nc = self.nc
        es = ExitStack()
        engs = ["pe", "act", "dve", "pool", "sp"]
        csem = {e: es.enter_context(nc.semaphore("c_" + e)) for e in engs}
        dsem = {e: [es.enter_context(nc.semaphore("d_%s%d" % (e, i))) for i in range(self.NDMASEM)]
                for e in ("act", "pool", "sp")}
        ccount = {e: 0 for e in engs}
        dcount = {e: 0 for e in dsem}
        lim = DBG.get('limit')
        if lim is not None:
            self.ops = self.ops[:lim]
            self.final = [o for o in self.final if o.idx < lim]
        if self.final:
            fin = Op("sp", None, False)
            fin.deps = set(self.final)
            for d in fin.deps:
                d.users += 1
            fin.idx = len(self.ops)
            self.ops.append(fin)
        for o in self.ops:
            waits = {}
            if o.dma:
                i = dcount[o.eng]
                dcount[o.eng] += 1
                s = dsem[o.eng][i % self.NDMASEM]
                rnd = i // self.NDMASEM
                o.sig = (s, 16 * (rnd + 1))
                if rnd > 0:
                    waits[id(s)] = (s, 16 * rnd)
            elif o.users > 0:
                ccount[o.eng] += 1
                o.sig = (csem[o.eng], ccount[o.eng])
                o.inc = True
            for d in o.deps:
                s, v = d.sig
                if id(s) not in waits or waits[id(s)][1] < v:
                    waits[id(s)] = (s, v)
            o.waits = list(waits.values())
        with nc.Block() as block:
            for name, deco in (("pe", block.tensor), ("act", block.scalar), ("dve", block.vector),
                               ("pool", block.gpsimd), ("sp", block.sync)):
                ops = [o for o in self.ops if o.eng == name]
                if not ops:
                    continue

                def body(e, ops=ops):
                    waited = {}
                    for o in ops:
                        for (s, v) in o.waits:
                            if waited.get(id(s), 0) < v:
                                e.wait_ge(s, v)
                                waited[id(s)] = v
                        if o.fn is None:
                            continue
                        ins = o.fn(e)
                        if o.dma:
                            ins.then_inc(o.sig[0], 16)
                        elif o.inc:
                            ins.then_inc(o.sig[0], 1)

                deco(body)
        es.close()


class Ctx:
    pass


def make_ctx(nc, es):
    c = Ctx()
    c.nc = nc

    def sb(name, shape, dt):
        return es.enter_context(nc.sbuf_tensor("sb_" + name, shape, dt))

    c.sb = sb
    c.big = sb("big", [128, KC, TOK], F32)
    c.r_big = [Res("big%d" % i) for i in range(KC)]
    c.h = sb("h", [128, KC, TOK], BF16)
    c.r_h = [Res("h%d" % i) for i in range(KC)]
    c.ones = sb("ones", [128, 128], BF16)
    c.r_ones = Res("ones")
    c.rstd = sb("rstd", [128, TOK], F32)
    c.r_rstd = Res("rstd")
    c.sq = [sb("sq%d" % i, [128, TOK], BF16) for i in range(2)]
    c.r_sq = [Res("sq%d" % i) for i in range(2)]
    c.ps = [es.enter_context(nc.psum_tensor("ps%d" % i, [128, 512], F32)) for i in range(8)]
    c.r_ps = [Res("ps%d" % i, excl=True) for i in range(8)]
    c.gains = sb("gains", [128, 64], F32)
    c.r_gains = Res("gains")
    return c


def load_consts(P, c):
    P.add("dve", lambda e: e.memset(c.ones[:], 1.0), w=[c.r_ones])


def rms_stats(P, c, src_chunks, r_src, nchunks, dim, out_rstd, r_out, ps_banks=(0, 1)):
    for i in range(nchunks):
        s = i % 2
        P.add("act", lambda e, i=i, s=s: e.activation(out=c.sq[s][:], in_=src_chunks(i), func=AF.Square),
              r=[r_src[i]], w=[c.r_sq[s]])
        for b in range(2):
            P.add("pe", lambda e, i=i, s=s, b=b: e.matmul(
                out=c.ps[ps_banks[b]][:], lhsT=c.ones[:], rhs=c.sq[s][:, b * 512:(b + 1) * 512],
                start=(i == 0), stop=(i == nchunks - 1)),
                r=[c.r_sq[s], c.r_ones], w=[c.r_ps[ps_banks[b]]])
    for b in range(2):
        P.add("dve", lambda e, b=b: e.tensor_scalar(
            out=out_rstd[:, b * 512:(b + 1) * 512], in0=c.ps[ps_banks[b]][:],
            scalar1=1.0 / dim, scalar2=EPS, op0=ALU.mult, op1=ALU.add),
            r=[c.r_ps[ps_banks[b]]], w=[r_out])
    P.add("act", lambda e: e.activation(out=out_rstd[:], in_=out_rstd[:], func=AF.Sqrt),
          r=[r_out], w=[r_out])
    P.add("dve", lambda e: e.reciprocal(out=out_rstd[:], in_=out_rstd[:]),
          r=[r_out], w=[r_out])


NFS = 4
FPS = FC // NFS


def alloc_ffn(c):
    sb = c.sb
    c.act = sb("ffn_act", [128, FPS, TOK], BF16)
    c.r_act = [Res("act%d" % i) for i in range(FPS)]
    c.wgu = [sb("wgu%d" % i, [128, 2, KC, 128], BF16) for i in range(3)]
    c.r_wgu = [Res("wgu%d" % i) for i in range(3)]
    c.wd = [sb("wd%d" % i, [128, FPS, 128], BF16) for i in range(3)]
    c.r_wd = [Res("wd%d" % i) for i in range(3)]
    c.sil = [sb("sil%d" % i, [128, 512], F32) for i in range(4)]
    c.r_sil = [Res("sil%d" % i) for i in range(4)]
    c.xb = [sb("xb%d" % i, [128, TOK], F32) for i in range(3)]
    c.r_xb = [Res("xb%d" % i) for i in range(3)]
    c.yb = [sb("yb%d" % i, [128, TOK], F32) for i in range(2)]
    c.r_yb = [Res("yb%d" % i) for i in range(2)]
    c.ghalf = sb("ghalf", [128, KC], F32)
    c.r_ghalf = Res("ghalf")


def ffn_phase(P, c, x_in, x_out, wg, wu, wd, gcol_pre, gcol_post, final=False):
    xin_v = x_in.rearrange("(kc kp) t -> kp kc t", kp=128)
    xout_v = x_out.rearrange("(kc kp) t -> kp kc t", kp=128)
    for q in range(4):
        P.add("sp", lambda e, q=q: e.dma_start(out=c.big[:, 4 * q:4 * q + 4, :], in_=xin_v[:, 4 * q:4 * q + 4, :]),
              w=c.r_big[4 * q:4 * q + 4], dma=True)
    rms_stats(P, c, lambda i: c.big[:, i, :], c.r_big, KC, D, c.rstd, c.r_rstd)
    for i in range(KC):
        P.add("dve", lambda e, i=i: e.scalar_tensor_tensor(
            out=c.h[:, i, :], in0=c.big[:, i, :], scalar=c.gains[:, gcol_pre + i:gcol_pre + i + 1],
            in1=c.rstd[:], op0=ALU.mult, op1=ALU.mult),
            r=[c.r_big[i], c.r_rstd, c.r_gains], w=[c.r_h[i]])
    P.add("dve", lambda e: e.tensor_scalar(out=c.ghalf[:], in0=c.gains[:, gcol_post:gcol_post + KC],
                                           scalar1=0.5, scalar2=None, op0=ALU.mult),
          r=[c.r_gains], w=[c.r_ghalf])
    wslot = 0
    dslot = 0
    bank = 0
    for fs in range(NFS):
        for jc in range(FPS):
            f0 = (fs * FPS + jc) * 128
            ws = wslot % 3
            wslot += 1
            for t, wsrc in enumerate((wg, wu)):
                P.add("pool", lambda e, ws=ws, t=t, wsrc=wsrc, f0=f0: e.dma_start(
                    out=c.wgu[ws][:, t, :, :],
                    in_=wsrc[:, f0:f0 + 128].rearrange("(kc kp) f -> kp kc f", kp=128)),
                    w=[c.r_wgu[ws]], dma=True)
            pb = (jc % 2) * 4
            for t in range(2):
                for k in range(KC):
                    for b in range(2):
                        P.add("pe", lambda e, ws=ws, t=t, k=k, b=b, pb=pb: e.matmul(
                            out=c.ps[pb + 2 * t + b][:], lhsT=c.wgu[ws][:, t, k, :],
                            rhs=c.h[:, k, b * 512:(b + 1) * 512], start=(k == 0), stop=(k == KC - 1)),
                            r=[c.r_wgu[ws], c.r_h[k]], w=[c.r_ps[pb + 2 * t + b]])
            for b in range(2):
                sl = (2 * jc + b) % 4
                P.add("act", lambda e, sl=sl, pb=pb, b=b: e.activation(
                    out=c.sil[sl][:], in_=c.ps[pb + b][:], func=AF.Silu),
                    r=[c.r_ps[pb + b]], w=[c.r_sil[sl]])
                P.add("dve", lambda e, sl=sl, pb=pb, b=b, jc=jc: e.tensor_tensor(
                    out=c.act[:, jc, b * 512:(b + 1) * 512], in0=c.sil[sl][:], in1=c.ps[pb + 2 + b][:],
                    op=ALU.mult),
                    r=[c.r_sil[sl], c.r_ps[pb + 2 + b]], w=[c.r_act[jc]])
        for i in range(KC if DBG.get('p2', 1) else 0):
            ds_ = dslot % 3
            dslot += 1
            r0 = fs * FPS * 128
            P.add("pool", lambda e, ds_=ds_, i=i, r0=r0: e.dma_start(
                out=c.wd[ds_][:],
                in_=wd[r0:r0 + FPS * 128, i * 128:(i + 1) * 128].rearrange("(jc p) d -> p jc d", p=128)),
                w=[c.r_wd[ds_]], dma=True)
            for b in range(2):
                bk = bank % 8
                bank += 1
                for jc in range(FPS):
                    P.add("pe", lambda e, ds_=ds_, jc=jc, b=b, bk=bk: e.matmul(
                        out=c.ps[bk][:], lhsT=c.wd[ds_][:, jc, :], rhs=c.act[:, jc, b * 512:(b + 1) * 512],
                        start=(jc == 0), stop=(jc == FPS - 1)),
                        r=[c.r_wd[ds_], c.r_act[jc]], w=[c.r_ps[bk]])
                if fs == 0:
                    P.add("act", lambda e, i=i, b=b, bk=bk: e.copy(
                        out=c.big[:, i, b * 512:(b + 1) * 512], in_=c.ps[bk][:]),
                        r=[c.r_ps[bk]], w=[c.r_big[i]])
                else:
                    P.add("dve", lambda e, i=i, b=b, bk=bk: e.tensor_tensor(
                        out=c.big[:, i, b * 512:(b + 1) * 512], in0=c.big[:, i, b * 512:(b + 1) * 512],
                        in1=c.ps[bk][:], op=ALU.add),
                        r=[c.r_ps[bk], c.r_big[i]], w=[c.r_big[i]])
    if DBG.get('post', 1):
        rms_stats(P, c, lambda i: c.big[:, i, :], c.r_big, KC, D, c.rstd, c.r_rstd)
    else:
        for i in range(KC):
            P.add('sp', lambda e, i=i: e.dma_start(out=xout_v[:, i, :], in_=c.big[:, i, :]), r=[c.r_big[i]], dma=True, final=final)
    for i in range(KC if DBG.get('post', 1) else 0):
        xs = i % 3
        ys = i % 2
        P.add("sp", lambda e, i=i, xs=xs: e.dma_start(out=c.xb[xs][:], in_=xin_v[:, i, :]),
              w=[c.r_xb[xs]], dma=True)
        P.add("dve", lambda e, i=i, ys=ys: e.scalar_tensor_tensor(
            out=c.yb[ys][:], in0=c.big[:, i, :], scalar=c.ghalf[:, i:i + 1], in1=c.rstd[:],
            op0=ALU.mult, op1=ALU.mult),
            r=[c.r_big[i], c.r_ghalf, c.r_rstd], w=[c.r_yb[ys]])
        P.add("dve", lambda e, xs=xs, ys=ys: e.tensor_tensor(
            out=c.yb[ys][:], in0=c.yb[ys][:], in1=c.xb[xs][:], op=ALU.add),
            r=[c.r_yb[ys], c.r_xb[xs]], w=[c.r_yb[ys]])
        P.add("sp", lambda e, i=i, ys=ys: e.dma_start(out=xout_v[:, i, :], in_=c.yb[ys][:]),
              r=[c.r_yb[ys]], dma=True, final=final)


def build_ffn_launch():
    nc = bass.Bass("TRN2", target_bir_lowering=False)
    x_in = nc.dram_tensor("x_in", [D, TOK], F32, kind="ExternalInput").ap()
    wg = nc.dram_tensor("wg", [D, DFF], F32, kind="ExternalInput").ap()
    wu = nc.dram_tensor("wu", [D, DFF], F32, kind="ExternalInput").ap()
    wd = nc.dram_tensor("wd", [DFF, D], F32, kind="ExternalInput").ap()
    gains = nc.dram_tensor("gains", [128, 64], F32, kind="ExternalInput").ap()
    x_out = nc.dram_tensor("x_out", [D, TOK], F32, kind="ExternalOutput").ap()
    with ExitStack() as es:
        c = make_ctx(nc, es)
        alloc_ffn(c)
        P = Prog(nc)
        load_consts(P, c)
        P.add("sp", lambda e: e.dma_start(out=c.gains[:], in_=gains), w=[c.r_gains], dma=True)
        ffn_phase(P, c, x_in, x_out, wg, wu, wd, 0, KC, final=True)
        P.emit()
    return nc


def tok_index(core):
    a = np.arange(512 * core, 512 * core + 512)
    b = np.arange(512 * (15 - core), 512 * (15 - core) + 512)
    return np.concatenate([a, b])


def pack_gain(g):
    return np.ascontiguousarray(g.reshape(KC, 128).T)


PI = float(np.pi)
NH = 8
W_IN_COLS = 4160
G_MIXPRE, G_QN, G_KVN, G_ROPE, G_MIXPOST = 0, 16, 20, 32, 40


def alloc_proj(c):
    sb = c.sb
    c.wp = [sb("wp%d" % i, [128, KC, 128], BF16) for i in range(3)]
    c.r_wp = [Res("wp%d" % i) for i in range(3)]
    c.wv = [sb("wv%d" % i, [128, KC, 512], BF16) for i in range(1)]
    c.r_wv = [Res("wv%d" % i) for i in range(1)]
    c.st = [sb("st%d" % i, [128, TOK], BF16) for i in range(3)]
    c.r_st = [Res("st%d" % i) for i in range(3)]
    c.t1 = [sb("t1_%d" % i, [64, 512], F32) for i in range(2)]
    c.r_t1 = [Res("t1_%d" % i) for i in range(2)]
    c.t2 = [sb("t2_%d" % i, [64, 512], F32) for i in range(2)]
    c.r_t2 = [Res("t2_%d" % i) for i in range(2)]
    c.lat = sb("lat", [128, 8, TOK], BF16)
    c.r_lat = [Res("lat%d" % i) for i in range(8)]
    c.posi = sb("posi", [128, TOK], I32)
    c.tabs = {k: sb("tab_" + k, [64, TOK], F32) for k in ("cosA", "sinA", "cosP", "sinP")}
    c.r_tabs = {k: Res("tab_" + k) for k in c.tabs}
    c.ta = sb("ta", [64, TOK], F32)
    c.tb = sb("tb", [64, TOK], F32)
    c.tk = sb("tk", [64, TOK], I32)
    c.r_ta, c.r_tb, c.r_tk = Res("ta"), Res("tb"), Res("tk")
    c.vst = [sb("vst%d" % i, [128, 1024], BF16) for i in range(2)]
    c.r_vst = [Res("vst%d" % i) for i in range(2)]


def rope_tables(P, c, pos_rep):
    r_pos = Res("posi")
    P.add("sp", lambda e: e.dma_start(out=c.posi[:], in_=pos_rep), w=[r_pos], dma=True)
    for nm, nrows, col in (("A", 64, G_ROPE), ("P", 32, G_ROPE + 2)):
        P.add("dve", lambda e, nrows=nrows: e.tensor_copy(out=c.ta[0:nrows, :], in_=c.posi[0:nrows, :]),
              r=[r_pos], w=[c.r_ta])
        P.add("dve", lambda e, nrows=nrows, col=col: e.tensor_scalar(
            out=c.ta[0:nrows, :], in0=c.ta[0:nrows, :], scalar1=c.gains[0:nrows, col:col + 1], scalar2=None,
            op0=ALU.mult), r=[c.r_ta, c.r_gains], w=[c.r_ta])
        for kind, shift in (("sin", 0.0), ("cos", PI / 2)):
            tab = c.tabs[kind + nm]
            rt = c.r_tabs[kind + nm]
            P.add("dve", lambda e, nrows=nrows, shift=shift: e.tensor_scalar(
                out=c.tb[0:nrows, :], in0=c.ta[0:nrows, :], scalar1=shift, scalar2=None, op0=ALU.add),
                r=[c.r_ta], w=[c.r_tb])
            P.add("dve", lambda e, nrows=nrows, tab=tab: e.tensor_scalar(
                out=tab[0:nrows, :], in0=c.tb[0:nrows, :], scalar1=1.0 / (2 * PI), scalar2=None, op0=ALU.mult),
                r=[c.r_tb], w=[rt])
            P.add("dve", lambda e, nrows=nrows, tab=tab: e.tensor_copy(out=c.tk[0:nrows, :], in_=tab[0:nrows, :]),
                  r=[rt], w=[c.r_tk])
            P.add("dve", lambda e, nrows=nrows, tab=tab: e.tensor_copy(out=tab[0:nrows, :], in_=c.tk[0:nrows, :]),
                  r=[c.r_tk], w=[rt])
            P.add("dve", lambda e, nrows=nrows, tab=tab: e.scalar_tensor_tensor(
                out=c.tb[0:nrows, :], in0=tab[0:nrows, :], scalar=-2 * PI, in1=c.tb[0:nrows, :],
                op0=ALU.mult, op1=ALU.add), r=[rt, c.r_tb], w=[c.r_tb])
            P.add("dve", lambda e, nrows=nrows, tab=tab: e.tensor_scalar(
                out=tab[0:nrows, :], in0=c.tb[0:nrows, :], scalar1=PI, scalar2=-2 * PI, op0=ALU.is_gt, op1=ALU.mult),
                r=[c.r_tb], w=[rt])
            P.add("dve", lambda e, nrows=nrows, tab=tab: e.tensor_tensor(
                out=c.tb[0:nrows, :], in0=c.tb[0:nrows, :], in1=tab[0:nrows, :], op=ALU.add),
                r=[rt, c.r_tb], w=[c.r_tb])
            P.add("dve", lambda e, nrows=nrows, tab=tab: e.tensor_scalar(
                out=tab[0:nrows, :], in0=c.tb[0:nrows, :], scalar1=-PI, scalar2=2 * PI, op0=ALU.is_lt, op1=ALU.mult),
                r=[c.r_tb], w=[rt])
            P.add("dve", lambda e, nrows=nrows, tab=tab: e.tensor_tensor(
                out=c.tb[0:nrows, :], in0=c.tb[0:nrows, :], in1=tab[0:nrows, :], op=ALU.add),
                r=[rt, c.r_tb], w=[c.r_tb])
            P.add("act", lambda e, nrows=nrows, tab=tab: e.activation(out=tab[0:nrows, :], in_=c.tb[0:nrows, :], func=AF.Sin),
                  r=[c.r_tb], w=[rt])
            if kind == "sin":
                P.add("dve", lambda e, nrows=nrows, tab=tab, col=col: e.tensor_scalar(
                    out=tab[0:nrows, :], in0=tab[0:nrows, :], scalar1=c.gains[0:nrows, col + 1:col + 2], scalar2=None,
                    op0=ALU.mult), r=[rt, c.r_gains], w=[rt])


class Rot:
    def __init__(self):
        self.w = 0
        self.bank = 0
        self.st = 0
        self.t = 0
        self.v = 0
        self.vs = 0


def lin_fm(P, c, rot, wsrc, c0, M, nk, rhs, r_rhs, evac):
    ws = rot.w % 3
    rot.w += 1
    P.add("pool", lambda e: e.dma_start(
        out=c.wp[ws][:, 0:nk, 0:M], in_=wsrc[:, c0:c0 + M].rearrange("(kc kp) f -> kp kc f", kp=128)),
        w=[c.r_wp[ws]], dma=True)
    banks = []
    for b in range(2):
        bk = rot.bank % 8
        rot.bank += 1
        banks.append(bk)
    for k in range(nk):
        for b in range(2):
            P.add("pe", lambda e, k=k, b=b: e.matmul(
                out=c.ps[banks[b]][0:M, :], lhsT=c.wp[ws][:, k, 0:M], rhs=rhs[:, k, b * 512:(b + 1) * 512],
                start=(k == 0), stop=(k == nk - 1)),
                r=[c.r_wp[ws], r_rhs[k]], w=[c.r_ps[banks[b]]])
    for b in range(2):
        evac(b, banks[b])
    return banks


def proj_phase(P, c, x_in, pos_rep, w_in, w_sw, w_uq, w_uq_sw, w_ukv, w_vd, w_va, outs):
    rot = Rot()
    xin_v = x_in.rearrange("(kc kp) t -> kp kc t", kp=128)
    for q in range(4):
        P.add("sp", lambda e, q=q: e.dma_start(out=c.big[:, 4 * q:4 * q + 4, :], in_=xin_v[:, 4 * q:4 * q + 4, :]),
              w=c.r_big[4 * q:4 * q + 4], dma=True)
    rope_tables(P, c, pos_rep)
    rms_stats(P, c, lambda i: c.big[:, i, :], c.r_big, KC, D, c.rstd, c.r_rstd)
    for i in range(KC):
        P.add("dve", lambda e, i=i: e.scalar_tensor_tensor(
            out=c.h[:, i, :], in0=c.big[:, i, :], scalar=c.gains[:, G_MIXPRE + i:G_MIXPRE + i + 1],
            in1=c.rstd[:], op0=ALU.mult, op1=ALU.mult),
            r=[c.r_big[i], c.r_rstd, c.r_gains], w=[c.r_h[i]])

    def store(dst, rows, st):
        P.add("sp", lambda e: e.dma_start(out=dst, in_=c.st[st][0:rows, :]), r=[c.r_st[st]], dma=True, final=True)

    def evac_plain(dst, rows):
        st = rot.st % 3
        rot.st += 1

        def ev(b, bk):
            P.add("act", lambda e: e.copy(out=c.st[st][0:rows, b * 512:(b + 1) * 512], in_=c.ps[bk][0:rows, :]),
                  r=[c.r_ps[bk]], w=[c.r_st[st]])
            if b == 1:
                store(dst, rows, st)
        return ev

    def rope_chunk(wmain, cm, M, wswp, cs, R, nk, rhs, r_rhs, tabn, dst):
        st = rot.st % 3
        rot.st += 1
        held = {}

        def ev_main(b, bk):
            held[b] = bk
        lin_fm(P, c, rot, wmain, cm, M, nk, rhs, r_rhs, ev_main)

        def ev_sw(b, bk2):
            bk = held[b]
            t = rot.t % 2
            rot.t += 1
            cs_, sn_ = c.tabs["cos" + tabn], c.tabs["sin" + tabn]
            P.add("dve", lambda e: e.tensor_tensor(
                out=c.t1[t][0:R, :], in0=c.ps[bk][0:R, :], in1=cs_[0:R, b * 512:(b + 1) * 512], op=ALU.mult),
                r=[c.r_ps[bk], c.r_tabs["cos" + tabn]], w=[c.r_t1[t]])
            P.add("dve", lambda e: e.tensor_tensor(
                out=c.t2[t][0:R, :], in0=c.ps[bk2][0:R, :], in1=sn_[0:R, b * 512:(b + 1) * 512], op=ALU.mult),
                r=[c.r_ps[bk2], c.r_tabs["sin" + tabn]], w=[c.r_t2[t]])
            if M > R:
                P.add("act", lambda e: e.copy(out=c.st[st][0:M, b * 512:(b + 1) * 512], in_=c.ps[bk][0:M, :]),
                      r=[c.r_ps[bk]], w=[c.r_st[st]])
            P.add("dve", lambda e: e.tensor_tensor(
                out=c.st[st][0:R, b * 512:(b + 1) * 512], in0=c.t1[t][0:R, :], in1=c.t2[t][0:R, :], op=ALU.add),
                r=[c.r_t1[t], c.r_t2[t]], w=[c.r_st[st]])
            if b == 1:
                store(dst, M, st)
        lin_fm(P, c, rot, wswp, cs, R, nk, rhs, r_rhs, ev_sw)

    for j in range(8):
        def ev(b, bk, j=j):
            P.add("act", lambda e: e.copy(out=c.big[:, j, b * 512:(b + 1) * 512], in_=c.ps[bk][:]),
                  r=[c.r_ps[bk]] + c.r_h, w=[c.r_big[j]])
        lin_fm(P, c, rot, w_in, j * 128, 128, KC, c.h, c.r_h, ev)
    for grp, gcol in ((0, G_QN), (1, G_KVN)):
        rms_stats(P, c, lambda i, grp=grp: c.big[:, 4 * grp + i, :], c.r_big[4 * grp:4 * grp + 4], 4, 512,
                  c.rstd, c.r_rstd, ps_banks=(2 * grp, 2 * grp + 1))
        for i in range(4):
            P.add("dve", lambda e, i=i, grp=grp, gcol=gcol: e.scalar_tensor_tensor(
                out=c.lat[:, 4 * grp + i, :], in0=c.big[:, 4 * grp + i, :], scalar=c.gains[:, gcol + i:gcol + i + 1],
                in1=c.rstd[:], op0=ALU.mult, op1=ALU.mult),
                r=[c.r_big[4 * grp + i], c.r_rstd, c.r_gains], w=[c.r_lat[4 * grp + i]])
    latq, r_latq = c.lat[:, 0:4, :], c.r_lat[0:4]
    latkv, r_latkv = c.lat[:, 4:8, :], c.r_lat[4:8]
    rope_chunk(w_in, 1024, 64, w_sw, 0, 64, KC, c.h, c.r_h, "A", outs["kr"])
    for hd in range(NH):
        rope_chunk(w_in, 1088 + hd * 128, 128, w_sw, 64 + hd * 32, 32, KC, c.h, c.r_h, "P", outs["qd"][hd])
        rope_chunk(w_in, 2112 + hd * 128, 128, w_sw, 64 + 256 + hd * 32, 32, KC, c.h, c.r_h, "P", outs["kd"][hd])
    for hd in range(NH):
        lin_fm(P, c, rot, w_uq, hd * 192, 128, 4, latq, r_latq, evac_plain(outs["qn"][hd], 128))
        rope_chunk(w_uq, hd * 192 + 128, 64, w_uq_sw, hd * 64, 64, 4, latq, r_latq, "A", outs["qr"][hd])
        lin_fm(P, c, rot, w_ukv, hd * 256, 128, 4, latkv, r_latkv, evac_plain(outs["kn"][hd], 128))
    for (wsrc, nk, lhs, r_lhs, dst) in ((w_vd, KC, c.h, c.r_h, outs["vd"]), (w_va, 4, latkv, r_latkv, outs["va"])):
        for half in range(2):
            v = 0
            P.add("pool", lambda e, v=v, wsrc=wsrc, nk=nk, half=half: e.dma_start(
                out=c.wv[v][:, 0:nk, :],
                in_=wsrc[:, half * 512:(half + 1) * 512].rearrange("(kc kp) f -> kp kc f", kp=128)),
                w=[c.r_wv[v]], dma=True)
            for tt in range(TOK // 128):
                bk = rot.bank % 8
                rot.bank += 1
                for k in range(nk):
                    P.add("pe", lambda e, k=k, tt=tt, bk=bk, v=v, lhs=lhs, nk=nk: e.matmul(
                        out=c.ps[bk][:], lhsT=lhs[:, k, tt * 128:(tt + 1) * 128], rhs=c.wv[v][:, k, :],
                        start=(k == 0), stop=(k == nk - 1)),
                        r=[c.r_wv[v], r_lhs[k]], w=[c.r_ps[bk]])
                vs = rot.vs % 2
                rot.vs += 1
                P.add("act", lambda e, bk=bk, vs=vs: e.copy(out=c.vst[vs][:, 0:512], in_=c.ps[bk][:]),
                      r=[c.r_ps[bk]], w=[c.r_vst[vs]])
                P.add("sp", lambda e, vs=vs, tt=tt, half=half, dst=dst: e.dma_start(
                    out=dst[tt * 128:(tt + 1) * 128, half * 512:(half + 1) * 512], in_=c.vst[vs][:, 0:512]),
                    r=[c.r_vst[vs]], dma=True, final=True)


PROJ_OUT = {"qn": [NH, 128, TOK], "qr": [NH, 64, TOK], "kn": [NH, 128, TOK], "kr": [64, TOK],
            "va": [TOK, 1024], "qd": [NH, 128, TOK], "kd": [NH, 128, TOK], "vd": [TOK, 1024]}


def build_proj_launch():
    nc = bass.Bass("TRN2", target_bir_lowering=False)
    dt_ = lambda n, s, d=F32: nc.dram_tensor(n, s, d, kind="ExternalInput").ap()
    x_in = dt_("x_in", [D, TOK])
    pos = dt_("pos", [128, TOK], I32)
    w_in = dt_("w_in", [D, W_IN_COLS])
    w_sw = dt_("w_sw", [D, 576])
    w_uq = dt_("w_uq", [512, 1536])
    w_uq_sw = dt_("w_uq_sw", [512, 512])
    w_ukv = dt_("w_ukv", [512, 2048])
    w_vd = dt_("w_vd", [D, 1024])
    w_va = dt_("w_va", [512, 1024])
    gains = dt_("gains", [128, 64])
    outs = {k: nc.dram_tensor("o_" + k, s, BF16, kind="ExternalOutput").ap() for k, s in PROJ_OUT.items()}
    with ExitStack() as es:
        c = make_ctx(nc, es)
        alloc_proj(c)
        P = Prog(nc)
        load_consts(P, c)
        P.add("sp", lambda e: e.dma_start(out=c.gains[:], in_=gains), w=[c.r_gains], dma=True)
        proj_phase(P, c, x_in, pos, w_in, w_sw, w_uq, w_uq_sw, w_ukv, w_vd, w_va, outs)
        P.emit()
    return nc


def rope_consts():
    g = np.zeros((128, 8), np.float32)
    inv_a = (500000.0 ** (-np.arange(0, 64, 2, dtype=np.float32) / 64)).astype(np.float32)
    inv_p = (500000.0 ** (-np.arange(0, 32, 2, dtype=np.float32) / 32)).astype(np.float32)
    g[0:64, 0] = np.concatenate([inv_a, inv_a])
    g[0:64, 1] = np.concatenate([-np.ones(32), np.ones(32)])
    g[0:32, 2] = np.concatenate([inv_p, inv_p])
    g[0:32, 3] = np.concatenate([-np.ones(16), np.ones(16)])
    return g


def proj_weights(w_in, w_uq, w_ukv):
    sw = [w_in[:, 1024 + 32:1024 + 64], w_in[:, 1024:1024 + 32]]
    for base in (1088, 2112):
        for hd in range(NH):
            c0 = base + hd * 128
            sw += [w_in[:, c0 + 16:c0 + 32], w_in[:, c0:c0 + 16]]
    w_sw = np.ascontiguousarray(np.concatenate(sw, axis=1))
    uqs = []
    for hd in range(NH):
        c0 = hd * 192 + 128
        uqs += [w_uq[:, c0 + 32:c0 + 64], w_uq[:, c0:c0 + 32]]
    w_uq_sw = np.ascontiguousarray(np.concatenate(uqs, axis=1))
    w_vd = np.ascontiguousarray(w_in[:, 3136:4160])
    w_va = np.ascontiguousarray(np.concatenate([w_ukv[:, hd * 256 + 128:hd * 256 + 256] for hd in range(NH)], axis=1))
    return w_sw, w_uq_sw, w_vd, w_va


NWIN = 20
MLA_TILES = (32, 64)


def alloc_attn(c):
    sb = c.sb
    c.kn = sb("kn", [128, S], BF16)
    c.kr = sb("kr", [64, S], BF16)
    c.va = sb("va", [128, 64, 128], BF16)
    c.r_kn, c.r_kr, c.r_va = Res("kn"), Res("kr"), Res("va")
    c.q1 = sb("q1", [128, TOK], BF16)
    c.q2 = sb("q2", [64, TOK], BF16)
    c.r_q1, c.r_q2 = Res("q1"), Res("q2")
    c.kdw = sb("kdw", [128, NWIN * 128], BF16)
    c.vdw = sb("vdw", [128, NWIN, 128], BF16)
    c.r_kdw, c.r_vdw = Res("kdw"), Res("vdw")
    c.pt = [sb("pt%d" % i, [128, 512], BF16) for i in range(3)]
    c.r_pt = [Res("pt%d" % i) for i in range(3)]
    c.pm = [sb("pm%d" % i, [128, 512], BF16) for i in range(3)]
    c.r_pm = [Res("pm%d" % i) for i in range(3)]
    c.dm = [sb("dm%d" % i, [128, 512], BF16) for i in range(3)]
    c.r_dm = [Res("dm%d" % i) for i in range(3)]
    c.qidx = sb("qidx", [128, TOK], F32)
    c.kidx = sb("kidx", [128, 64], F32)
    c.r_idx = Res("idx")
    c.rden = sb("rden", [128, 512], F32)
    c.r_rden = Res("rden")
    c.wp = [sb("wp%d" % i, [128, KC, 128], BF16) for i in range(2)]
    c.r_wp = [Res("wp%d" % i) for i in range(2)]
    c.xb = [sb("xb%d" % i, [128, TOK], F32) for i in range(2)]
    c.r_xb = [Res("xb%d" % i) for i in range(2)]
    c.yb = [sb("yb%d" % i, [128, TOK], F32) for i in range(2)]
    c.r_yb = [Res("yb%d" % i) for i in range(2)]


def attn_phase(P, c, x_in, x_out, A, w_o, final=True):
    xin_v = x_in.rearrange("(kc kp) t -> kp kc t", kp=128)
    xout_v = x_out.rearrange("(kc kp) t -> kp kc t", kp=128)
    P.add("sp", lambda e: e.dma_start(out=c.qidx[:], in_=A["qidx"]), w=[c.r_idx], dma=True)
    P.add("sp", lambda e: e.dma_start(out=c.kidx[:], in_=A["kidx"]), w=[c.r_idx], dma=True)
    P.add("sp", lambda e: e.dma_start(out=c.kr[:], in_=A["kr"]), w=[c.r_kr], dma=True)
    st = {"sb": 0, "p": 0, "acc": 0}
    SB = (0, 1, 6, 7)

    def core(ntile, score_mm, mask_op, v_lhsT, out_chunk, b):
        ob, db = (2, 3) if st["acc"] % 2 == 0 else (4, 5)
        st["acc"] += 1
        for kt in range(ntile):
            sbk = SB[st["sb"] % 4]
            st["sb"] += 1
            ps_ = st["p"] % 3
            st["p"] += 1
            score_mm(kt, sbk)
            P.add("act", lambda e, sbk=sbk, ps_=ps_: e.activation(
                out=c.pt[ps_][:], in_=c.ps[sbk][:], func=AF.Exp, scale=score_mm.scale),
                r=[c.r_ps[sbk]], w=[c.r_pt[ps_]])
            mask_op(kt, ps_)
            lhsT, r_l = v_lhsT(kt)
            P.add("pe", lambda e, ps_=ps_, lhsT=lhsT, kt=kt: e.matmul(
                out=c.ps[ob][:], lhsT=lhsT, rhs=c.pm[ps_][:], start=(kt == 0), stop=(kt == ntile - 1)),
                r=[r_l, c.r_pm[ps_]], w=[c.r_ps[ob]])
            P.add("pe", lambda e, ps_=ps_, kt=kt: e.matmul(
                out=c.ps[db][:], lhsT=c.ones[:], rhs=c.pm[ps_][:], start=(kt == 0), stop=(kt == ntile - 1)),
                r=[c.r_ones, c.r_pm[ps_]], w=[c.r_ps[db]])
        P.add("dve", lambda e: e.reciprocal(out=c.rden[:], in_=c.ps[db][:]), r=[c.r_ps[db]], w=[c.r_rden])
        P.add("dve", lambda e: e.tensor_tensor(
            out=c.h[:, out_chunk, b * 512:(b + 1) * 512], in0=c.ps[ob][:], in1=c.rden[:], op=ALU.mult),
            r=[c.r_ps[ob], c.r_rden], w=[c.r_h[out_chunk]])

    for hd in range(NH):
        P.add("sp", lambda e, hd=hd: e.dma_start(out=c.kn[:], in_=A["kn"][hd]), w=[c.r_kn], dma=True)
        P.add("sp", lambda e, hd=hd: e.dma_start(
            out=c.va[:], in_=A["va"][:, hd * 128:(hd + 1) * 128].rearrange("(kt p) v -> p kt v", p=128)),
            w=[c.r_va], dma=True)
        P.add("sp", lambda e, hd=hd: e.dma_start(out=c.q1[:], in_=A["qn"][hd]), w=[c.r_q1], dma=True)
        P.add("sp", lambda e, hd=hd: e.dma_start(out=c.q2[:], in_=A["qr"][hd]), w=[c.r_q2], dma=True)
        for b in range(2):
            def score_mm(kt, sbk, b=b):
                P.add("pe", lambda e: e.matmul(
                    out=c.ps[sbk][:], lhsT=c.kn[:, kt * 128:(kt + 1) * 128], rhs=c.q1[:, b * 512:(b + 1) * 512],
                    start=True, stop=False), r=[c.r_kn, c.r_q1], w=[c.r_ps[sbk]])
                P.add("pe", lambda e: e.matmul(
                    out=c.ps[sbk][:], lhsT=c.kr[0:64, kt * 128:(kt + 1) * 128], rhs=c.q2[0:64, b * 512:(b + 1) * 512],
                    start=False, stop=True), r=[c.r_kr, c.r_q2], w=[c.r_ps[sbk]])
            score_mm.scale = 1.0 / float(np.sqrt(192.0))

            def mask_op(kt, ps_, b=b):
                P.add("dve", lambda e: e.scalar_tensor_tensor(
                    out=c.pm[ps_][:], in0=c.qidx[:, b * 512:(b + 1) * 512], scalar=c.kidx[:, kt:kt + 1],
                    in1=c.pt[ps_][:], op0=ALU.is_ge, op1=ALU.mult),
                    r=[c.r_idx, c.r_pt[ps_]], w=[c.r_pm[ps_]])
            core(MLA_TILES[b], score_mm, mask_op, lambda kt: (c.va[:, kt, :], c.r_va), hd, b)
    dmi = [0]
    for hd in range(NH):
        P.add("sp", lambda e, hd=hd: e.dma_start(out=c.q1[:], in_=A["qd"][hd]), w=[c.r_q1], dma=True)
        for b in range(2):
            P.add("sp", lambda e, hd=hd, b=b: e.dma_start(out=c.kdw[:], in_=A["kdw"][hd, b]), w=[c.r_kdw], dma=True)
            P.add("sp", lambda e, hd=hd, b=b: e.dma_start(out=c.vdw[:], in_=A["vdw"][hd, b]), w=[c.r_vdw], dma=True)

            def score_mm(kt, sbk, b=b):
                P.add("pe", lambda e: e.matmul(
                    out=c.ps[sbk][:], lhsT=c.kdw[:, kt * 128:(kt + 1) * 128], rhs=c.q1[:, b * 512:(b + 1) * 512],
                    start=True, stop=True), r=[c.r_kdw, c.r_q1], w=[c.r_ps[sbk]])
            score_mm.scale = 1.0 / float(np.sqrt(128.0))

            def mask_op(kt, ps_, b=b):
                ds_ = dmi[0] % 3
                dmi[0] += 1
                P.add("pool", lambda e: e.dma_start(out=c.dm[ds_][:], in_=A["dmask"][b, kt]), w=[c.r_dm[ds_]], dma=True)
                P.add("dve", lambda e: e.tensor_tensor(
                    out=c.pm[ps_][:], in0=c.pt[ps_][:], in1=c.dm[ds_][:], op=ALU.mult),
                    r=[c.r_dm[ds_], c.r_pt[ps_]], w=[c.r_pm[ps_]])
            core(NWIN, score_mm, mask_op, lambda kt: (c.vdw[:, kt, :], c.r_vdw), NH + hd, b)
    rot = Rot()
    for i in range(KC):
        ws = rot.w % 2
        rot.w += 1
        P.add("pool", lambda e, ws=ws, i=i: e.dma_start(
            out=c.wp[ws][:], in_=w_o[:, i * 128:(i + 1) * 128].rearrange("(kc kp) f -> kp kc f", kp=128)),
            w=[c.r_wp[ws]], dma=True)
        for b in range(2):
            bk = rot.bank % 8
            rot.bank += 1
            for k in range(KC):
                P.add("pe", lambda e, ws=ws, k=k, b=b, bk=bk: e.matmul(
                    out=c.ps[bk][:], lhsT=c.wp[ws][:, k, :], rhs=c.h[:, k, b * 512:(b + 1) * 512],
                    start=(k == 0), stop=(k == KC - 1)),
                    r=[c.r_wp[ws], c.r_h[k]], w=[c.r_ps[bk]])
            P.add("act", lambda e, i=i, b=b, bk=bk: e.copy(out=c.big[:, i, b * 512:(b + 1) * 512], in_=c.ps[bk][:]),
                  r=[c.r_ps[bk]], w=[c.r_big[i]])
    rms_stats(P, c, lambda i: c.big[:, i, :], c.r_big, KC, D, c.rstd, c.r_rstd)
    for i in range(KC):
        xs = i % 2
        ys = i % 2
        P.add("sp", lambda e, i=i, xs=xs: e.dma_start(out=c.xb[xs][:], in_=xin_v[:, i, :]), w=[c.r_xb[xs]], dma=True)
        P.add("dve", lambda e, i=i, ys=ys: e.scalar_tensor_tensor(
            out=c.yb[ys][:], in0=c.big[:, i, :], scalar=c.gains[:, G_MIXPOST + i:G_MIXPOST + i + 1], in1=c.rstd[:],
            op0=ALU.mult, op1=ALU.mult),
            r=[c.r_big[i], c.r_gains, c.r_rstd], w=[c.r_yb[ys]])
        P.add("dve", lambda e, xs=xs, ys=ys: e.tensor_tensor(
            out=c.yb[ys][:], in0=c.yb[ys][:], in1=c.xb[xs][:], op=ALU.add),
            r=[c.r_yb[ys], c.r_xb[xs]], w=[c.r_yb[ys]])
        P.add("sp", lambda e, i=i, ys=ys: e.dma_start(out=xout_v[:, i, :], in_=c.yb[ys][:]),
              r=[c.r_yb[ys]], dma=True, final=final)


ATTN_IN = {"qn": ([NH, 128, TOK], BF16), "qr": ([NH, 64, TOK], BF16), "qd": ([NH, 128, TOK], BF16),
           "kn": ([NH, 128, S], BF16), "kr": ([64, S], BF16), "va": ([S, 1024], BF16),
           "kdw": ([NH, 2, 128, NWIN * 128], BF16), "vdw": ([NH, 2, 128, NWIN, 128], BF16),
           "dmask": ([2, NWIN, 128, 512], BF16), "qidx": ([128, TOK], F32), "kidx": ([128, 64], F32)}


def build_attn_launch():
    nc = bass.Bass("TRN2", target_bir_lowering=False)
    x_in = nc.dram_tensor("x_in", [D, TOK], F32, kind="ExternalInput").ap()
    w_o = nc.dram_tensor("w_o", [D, D], F32, kind="ExternalInput").ap()
    gains = nc.dram_tensor("gains", [128, 64], F32, kind="ExternalInput").ap()
    A = {k: nc.dram_tensor("a_" + k, s, d, kind="ExternalInput").ap() for k, (s, d) in ATTN_IN.items()}
    x_out = nc.dram_tensor("x_out", [D, TOK], F32, kind="ExternalOutput").ap()
    with ExitStack() as es:
        c = make_ctx(nc, es)
        alloc_attn(c)
        P = Prog(nc)
        load_consts(P, c)
        P.add("sp", lambda e: e.dma_start(out=c.gains[:], in_=gains), w=[c.r_gains], dma=True)
        attn_phase(P, c, x_in, x_out, A, w_o)
        P.emit()
    return nc


def attn_tables(core):
    tok = tok_index(core)
    qidx = np.ascontiguousarray(np.broadcast_to(tok.astype(np.float32), (128, TOK)))
    kidx = (128 * np.arange(64, dtype=np.float32)[None, :] + np.arange(128, dtype=np.float32)[:, None])
    dmask = np.zeros((2, NWIN, 128, 512), np.float32)
    for b in range(2):
        t0 = tok[b * 512]
        q = t0 + np.arange(512)
        for m in range(NWIN):
            k = t0 - 2048 + 128 * m + np.arange(128)
            dl = q[None, :] - k[:, None]
            ok = (dl >= 0) & (k[:, None] >= 0)
            cnt = (ok & (dl <= 128)).astype(np.float32) + (ok & (dl % 4 == 0) & (dl <= 512)) + (ok & (dl % 16 == 0) & (dl <= 2048))
            dmask[b, m] = cnt
    return qidx, np.ascontiguousarray(kidx), dmask


def dil_windows(core, kdG, vdG):
    tok = tok_index(core)
    kdw = np.zeros((NH, 2, 128, NWIN * 128), kdG.dtype)
    vdw = np.zeros((NH, 2, 128, NWIN, 128), vdG.dtype)
    for b in range(2):
        t0 = int(tok[b * 512])
        lo = t0 - 2048
        s0 = max(lo, 0)
        kdw[:, b, :, s0 - lo:] = kdG[:, :, s0:t0 + 512]
        vv = vdG[s0:t0 + 512].reshape(-1, NH, 128)
        full = np.zeros((NWIN * 128, NH, 128), vdG.dtype)
        full[s0 - lo:] = vv
        vdw[:, b] = full.reshape(NWIN, 128, NH, 128).transpose(2, 1, 0, 3)
    return kdw, vdw


_PROGS = {}


def _prog(name, builder):
    if name not in _PROGS:
        _PROGS[name] = builder()
    return _PROGS[name]


def _run(nc, in_maps):
    res = run_bass_kernel_spmd(nc, in_maps, core_ids=list(range(NCORE)))
    return res.results


def kernel(x, positions, ffn1_pre_g, ffn1_post_g, ffn1_w_gate, ffn1_w_up, ffn1_w_down,
           mix_pre_g, mix_post_g, w_in, mla_q_norm_g, mla_w_uq, mla_kv_norm_g, mla_w_ukv,
           w_o, ffn2_pre_g, ffn2_post_g, ffn2_w_gate, ffn2_w_up, ffn2_w_down):
    f32 = lambda a: np.ascontiguousarray(np.asarray(a), dtype=np.float32)
    x = f32(x)
    positions = np.asarray(positions)
    toks = [tok_index(c) for c in range(NCORE)]
    xs = [np.ascontiguousarray(x[0][toks[c]].T) for c in range(NCORE)]
    pos_rep = [np.ascontiguousarray(np.broadcast_to(positions[0][toks[c]].astype(np.int32), (128, TOK)))
               for c in range(NCORE)]
    tabs = [attn_tables(c) for c in range(NCORE)]
    rc = rope_consts()
    ffn_nc = _prog("ffn", build_ffn_launch)
    proj_nc = _prog("proj", build_proj_launch)
    attn_nc = _prog("attn", build_attn_launch)

    def run_ffn(xs, pre, post, wg, wu, wd):
        gains = np.zeros((128, 64), np.float32)
        gains[:, 0:KC] = pack_gain(f32(pre))
        gains[:, KC:2 * KC] = pack_gain(f32(post))
        wg, wu, wd = f32(wg), f32(wu), f32(wd)
        r = _run(ffn_nc, [{"x_in": xs[c], "wg": wg, "wu": wu, "wd": wd, "gains": gains} for c in range(NCORE)])
        return [r[c]["x_out"] for c in range(NCORE)]

    for l in range(DEPTH):
        xs = run_ffn(xs, ffn1_pre_g[l], ffn1_post_g[l], ffn1_w_gate[l], ffn1_w_up[l], ffn1_w_down[l])
        wl_in, wl_uq, wl_ukv = f32(w_in[l]), f32(mla_w_uq[l]), f32(mla_w_ukv[l])
        w_sw, w_uq_sw, w_vd, w_va = proj_weights(wl_in, wl_uq, wl_ukv)
        gains = np.zeros((128, 64), np.float32)
        gains[:, G_MIXPRE:G_MIXPRE + KC] = pack_gain(f32(mix_pre_g[l]))
        gains[:, G_QN:G_QN + 4] = f32(mla_q_norm_g[l]).reshape(4, 128).T
        gains[:, G_KVN:G_KVN + 4] = f32(mla_kv_norm_g[l]).reshape(4, 128).T
        gains[:, G_ROPE:G_ROPE + 8] = rc
        gains[:, G_MIXPOST:G_MIXPOST + KC] = pack_gain(f32(mix_post_g[l]))
        pr = _run(proj_nc, [{"x_in": xs[c], "pos": pos_rep[c], "w_in": wl_in, "w_sw": w_sw, "w_uq": wl_uq,
                             "w_uq_sw": w_uq_sw, "w_ukv": wl_ukv, "w_vd": w_vd, "w_va": w_va, "gains": gains}
                            for c in range(NCORE)])
        bdt = pr[0]["o_kn"].dtype
        knG = np.zeros((NH, 128, S), bdt)
        kdG = np.zeros((NH, 128, S), bdt)
        krG = np.zeros((64, S), bdt)
        vaG = np.zeros((S, 1024), bdt)
        vdG = np.zeros((S, 1024), bdt)
        for c in range(NCORE):
            knG[:, :, toks[c]] = pr[c]["o_kn"]
            kdG[:, :, toks[c]] = pr[c]["o_kd"]
            krG[:, toks[c]] = pr[c]["o_kr"]
            vaG[toks[c]] = pr[c]["o_va"]
            vdG[toks[c]] = pr[c]["o_vd"]
        wl_o = f32(w_o[l])
        maps = []
        for c in range(NCORE):
            kdw, vdw = dil_windows(c, kdG, vdG)
            qidx, kidx, dmask = tabs[c]
            maps.append({"x_in": xs[c], "w_o": wl_o, "gains": gains,
                         "a_qn": pr[c]["o_qn"], "a_qr": pr[c]["o_qr"], "a_qd": pr[c]["o_qd"],
                         "a_kn": knG, "a_kr": krG, "a_va": vaG, "a_kdw": kdw, "a_vdw": vdw,
                         "a_dmask": dmask.astype(bdt), "a_qidx": qidx, "a_kidx": kidx})
        ar = _run(attn_nc, maps)
        xs = [ar[c]["x_out"] for c in range(NCORE)]
        xs = run_ffn(xs, ffn2_pre_g[l], ffn2_post_g[l], ffn2_w_gate[l], ffn2_w_up[l], ffn2_w_down[l])
    out = np.zeros((1, S, D), np.float32)
    for c in range(NCORE):
        out[0, toks[c]] = xs[c].T
    return out
```

```python
import numpy as np
import concourse.bass as bass
import concourse.mybir as mybir
from concourse.bass_utils import run_bass_kernel_spmd
from contextlib import ExitStack

F32 = mybir.dt.float32
BF16 = mybir.dt.bfloat16
I32 = mybir.dt.int32
AF = mybir.ActivationFunctionType
ALU = mybir.AluOpType

D = 2048
S = 8192
DEPTH = 4
DFF = 5632
NCORE = 8
TOK = 1024
KC = D // 128
FC = DFF // 128
EPS = 1e-6
GW = 64 * 12 + 8
G_ROPEC = 64 * 12


DBG = {}


class Res:
    __slots__ = ("name", "last_w", "readers", "excl")
    ALL = []

    def __init__(self, name, excl=False):
        self.name = name
        self.last_w = None
        self.readers = {}
        self.excl = excl
        Res.ALL.append(self)


class Op:
    __slots__ = ("eng", "fn", "dma", "deps", "idx", "sig", "inc", "waits", "users", "incv")

    def __init__(self, eng, fn, dma):
        self.eng = eng
        self.fn = fn
        self.dma = dma
        self.deps = set()
        self.sig = None
        self.inc = False
        self.waits = []
        self.users = 0
        self.incv = 16


ENGS = ["pe", "act", "dve", "pool", "sp"]


class Prog:
    NDMASEM = 6

    def __init__(self, nc, es):
        self.nc = nc
        self.ops = []
        self.final = []
        self.emitted = 0
        self.csem = {e: es.enter_context(nc.semaphore("c_" + e)) for e in ENGS}
        self.dsem = {e: [es.enter_context(nc.semaphore("d_%s%d" % (e, i))) for i in range(self.NDMASEM)]
                     for e in ("act", "pool", "sp")}
        self.ccsem = [es.enter_context(nc.semaphore("cc%d" % i)) for i in range(2)]
        self.ccount = {e: 0 for e in ENGS}
        self.dcount = {e: 0 for e in self.dsem}
        self.dval = {}
        self.cccount = 0
        self.waited = {e: {} for e in ENGS}
        self.last = {e: None for e in ENGS}
        self.dmas = []

    def add(self, eng, fn, r=(), w=(), dma=False, final=False, cc=False):
        o = Op(eng, fn, dma or cc)
        if cc:
            o.incv = 1
        o.idx = len(self.ops)
        dma = o.dma
        deps = set()
        for res in r:
            d = res.last_w
            if d is not None:
                if d.dma or dma or d.eng != eng or eng != "pe":
                    deps.add(d)
            if res.excl:
                for d in res.readers.values():
                    if d.eng != eng:
                        deps.add(d)
        for res in w:
            d = res.last_w
            if d is not None and (d.dma or dma or d.eng != eng or eng != "pe"):
                deps.add(d)
            for d in res.readers.values():
                if d is not o and (d.dma or dma or d.eng != eng or eng != "pe"):
                    deps.add(d)
        for res in r:
            key = ("dma", o.idx) if dma else eng
            res.readers[key] = o
        for res in w:
            res.last_w = o
            res.readers = {}
        o.deps = deps
        for d in deps:
            d.users += 1
        self.ops.append(o)
        if dma:
            self.dmas.append(o)
        else:
            self.last[eng] = o
        if final:
            self.final.append(o)
        return o

    def barrier(self):
        tails = [o for o in self.last.values() if o is not None] + list(self.dmas)
        for e in ENGS:
            o = Op(e, None, False)
            o.idx = len(self.ops)
            o.deps = set(t for t in tails if t.eng != e or t.dma)
            for d in o.deps:
                d.users += 1
            self.ops.append(o)
        self.dmas = []
        for res in Res.ALL:
            res.last_w = None
            res.readers = {}

    def finish(self):
        fin = Op("sp", None, False)
        fin.deps = set(self.final)
        for d in fin.deps:
            d.users += 1
        fin.idx = len(self.ops)
        self.ops.append(fin)
        self.final = []

    def emit(self):
        nc = self.nc
        new = self.ops[self.emitted:]
        self.emitted = len(self.ops)
        for o in new:
            waits = {}
            if o.dma and o.incv == 1:
                s = self.ccsem[self.cccount % 2]
                self.cccount += 1
                prev = self.dval.get(id(s), 0)
                if prev > 0:
                    waits[id(s)] = (s, prev)
                self.dval[id(s)] = prev + 1
                o.sig = (s, prev + 1)
            elif o.dma:
                i = self.dcount[o.eng]
                self.dcount[o.eng] += 1
                s = self.dsem[o.eng][i % self.NDMASEM]
                prev = self.dval.get(id(s), 0)
                if prev > 0:
                    waits[id(s)] = (s, prev)
                self.dval[id(s)] = prev + 16
                o.sig = (s, prev + 16)
            elif o.users > 0 and o.fn is not None:
                self.ccount[o.eng] += 1
                o.sig = (self.csem[o.eng], self.ccount[o.eng])
                o.inc = True
            for d in o.deps:
                if d.sig is None:
                    continue
                s, v = d.sig
                if id(s) not in waits or waits[id(s)][1] < v:
                    waits[id(s)] = (s, v)
            o.waits = list(waits.values())
        with nc.Block() as block:
            for name, deco in (("pe", block.tensor), ("act", block.scalar), ("dve", block.vector),
                               ("pool", block.gpsimd), ("sp", block.sync)):
                ops = [o for o in new if o.eng == name]
                if not ops:
                    continue

                def body(e, ops=ops, name=name):
                    waited = self.waited[name]
                    for o in ops:
                        for (s, v) in o.waits:
                            if waited.get(id(s), 0) < v:
                                e.wait_ge(s, v)
                                waited[id(s)] = v
                        if o.fn is None:
                            continue
                        ins = o.fn(e)
                        if o.dma:
                            ins.then_inc(o.sig[0], o.incv)
                        elif o.inc:
                            ins.then_inc(o.sig[0], 1)

                deco(body)


class Ctx:
    pass


def make_ctx(nc, es):
    c = Ctx()
    c.nc = nc

    c.es = es
    c.uid = [0]

    def sb(name, shape, dt):
        c.uid[0] += 1
        return c.es.enter_context(nc.sbuf_tensor("sb%d_%s" % (c.uid[0], name), shape, dt))

    c.sb = sb
    c.gb = 0
    c.big = sb("big", [128, KC, TOK], F32)
    c.r_big = [Res("big%d" % i) for i in range(KC)]
    c.h = sb("h", [128, KC, TOK], BF16)
    c.r_h = [Res("h%d" % i) for i in range(KC)]
    c.ones = sb("ones", [128, 128], BF16)
    c.r_ones = Res("ones")
    c.rstd = sb("rstd", [128, TOK], F32)
    c.r_rstd = Res("rstd")
    c.sq = [sb("sq%d" % i, [128, TOK], BF16) for i in range(2)]
    c.r_sq = [Res("sq%d" % i) for i in range(2)]
    c.ps = [es.enter_context(nc.psum_tensor("ps%d" % i, [128, 512], F32)) for i in range(8)]
    c.r_ps = [Res("ps%d" % i, excl=True) for i in range(8)]
    c.gains = sb("gains", [128, GW], F32)
    c.tabs = {k: sb("tab_" + k, [64, TOK], F32) for k in ("cosA", "sinA", "cosP", "sinP")}
    c.r_tabs = {k: Res("tab_" + k) for k in c.tabs}
    c.r_gains = Res("gains")
    return c


def load_consts(P, c):
    P.add("dve", lambda e: e.memset(c.ones[:], 1.0), w=[c.r_ones])


def rms_stats(P, c, src_chunks, r_src, nchunks, dim, out_rstd, r_out, ps_banks=(0, 1)):
    for i in range(nchunks):
        s = i % 2
        P.add("act", lambda e, i=i, s=s: e.activation(out=c.sq[s][:], in_=src_chunks(i), func=AF.Square),
              r=[r_src[i]], w=[c.r_sq[s]])
        for b in range(2):
            P.add("pe", lambda e, i=i, s=s, b=b: e.matmul(
                out=c.ps[ps_banks[b]][:], lhsT=c.ones[:], rhs=c.sq[s][:, b * 512:(b + 1) * 512],
                start=(i == 0), stop=(i == nchunks - 1)),
                r=[c.r_sq[s], c.r_ones], w=[c.r_ps[ps_banks[b]]])
    for b in range(2):
        P.add("dve", lambda e, b=b: e.tensor_scalar(
            out=out_rstd[:, b * 512:(b + 1) * 512], in0=c.ps[ps_banks[b]][:],
            scalar1=1.0 / dim, scalar2=EPS, op0=ALU.mult, op1=ALU.add),
            r=[c.r_ps[ps_banks[b]]], w=[r_out])
    P.add("act", lambda e: e.activation(out=out_rstd[:], in_=out_rstd[:], func=AF.Sqrt),
          r=[r_out], w=[r_out])
    P.add("dve", lambda e: e.reciprocal(out=out_rstd[:], in_=out_rstd[:]),
          r=[r_out], w=[r_out])


NFS = 4
FPS = FC // NFS


def alloc_ffn(c):
    sb = c.sb
    c.act = sb("ffn_act", [128, FPS, TOK], BF16)
    c.r_act = [Res("act%d" % i) for i in range(FPS)]
    c.wgu = [sb("wgu%d" % i, [128, 2, KC, 128], BF16) for i in range(3)]
    c.r_wgu = [Res("wgu%d" % i) for i in range(3)]
    c.wd = [sb("wd%d" % i, [128, FPS, 128], BF16) for i in range(3)]
    c.r_wd = [Res("wd%d" % i) for i in range(3)]
    c.sil = [sb("sil%d" % i, [128, 512], F32) for i in range(4)]
    c.r_sil = [Res("sil%d" % i) for i in range(4)]
    c.xb = [sb("xb%d" % i, [128, TOK], F32) for i in range(2)]
    c.r_xb = [Res("xb%d" % i) for i in range(2)]
    c.yb = [sb("yb%d" % i, [128, TOK], F32) for i in range(2)]
    c.r_yb = [Res("yb%d" % i) for i in range(2)]
    c.ghalf = sb("ghalf", [128, KC], F32)
    c.r_ghalf = Res("ghalf")


def ffn_phase(P, c, x_in, x_out, wg, wu, wd, gcol_pre, gcol_post, final=False):
    xin_v = x_in.rearrange("(kc kp) t -> kp kc t", kp=128)
    xout_v = x_out.rearrange("(kc kp) t -> kp kc t", kp=128)
    for q in range(4):
        P.add("sp", lambda e, q=q: e.dma_start(out=c.big[:, 4 * q:4 * q + 4, :], in_=xin_v[:, 4 * q:4 * q + 4, :]),
              w=c.r_big[4 * q:4 * q + 4], dma=True)
    rms_stats(P, c, lambda i: c.big[:, i, :], c.r_big, KC, D, c.rstd, c.r_rstd)
    for i in range(KC):
        P.add("dve", lambda e, i=i: e.scalar_tensor_tensor(
            out=c.h[:, i, :], in0=c.big[:, i, :], scalar=c.gains[:, gcol_pre + i:gcol_pre + i + 1],
            in1=c.rstd[:], op0=ALU.mult, op1=ALU.mult),
            r=[c.r_big[i], c.r_rstd, c.r_gains], w=[c.r_h[i]])
    P.add("dve", lambda e: e.tensor_scalar(out=c.ghalf[:], in0=c.gains[:, gcol_post:gcol_post + KC],
                                           scalar1=0.5, scalar2=None, op0=ALU.mult),
          r=[c.r_gains], w=[c.r_ghalf])
    wslot = 0
    dslot = 0
    bank = 0
    for fs in range(NFS):
        for jc in range(FPS):
            f0 = (fs * FPS + jc) * 128
            ws = wslot % 3
            wslot += 1
            for t, wsrc in enumerate((wg, wu)):
                P.add("pool", lambda e, ws=ws, t=t, wsrc=wsrc, f0=f0: e.dma_start(
                    out=c.wgu[ws][:, t, :, :],
                    in_=wsrc[:, f0:f0 + 128].rearrange("(kc kp) f -> kp kc f", kp=128)),
                    w=[c.r_wgu[ws]], dma=True)
            pb = (jc % 2) * 4
            for t in range(2):
                for k in range(KC):
                    for b in range(2):
                        P.add("pe", lambda e, ws=ws, t=t, k=k, b=b, pb=pb: e.matmul(
                            out=c.ps[pb + 2 * t + b][:], lhsT=c.wgu[ws][:, t, k, :],
                            rhs=c.h[:, k, b * 512:(b + 1) * 512], start=(k == 0), stop=(k == KC - 1)),
                            r=[c.r_wgu[ws], c.r_h[k]], w=[c.r_ps[pb + 2 * t + b]])
            for b in range(2):
                sl = (2 * jc + b) % 4
                P.add("act", lambda e, sl=sl, pb=pb, b=b: e.activation(
                    out=c.sil[sl][:], in_=c.ps[pb + b][:], func=AF.Silu),
                    r=[c.r_ps[pb + b]], w=[c.r_sil[sl]])
                P.add("dve", lambda e, sl=sl, pb=pb, b=b, jc=jc: e.tensor_tensor(
                    out=c.act[:, jc, b * 512:(b + 1) * 512], in0=c.sil[sl][:], in1=c.ps[pb + 2 + b][:],
                    op=ALU.mult),
                    r=[c.r_sil[sl], c.r_ps[pb + 2 + b]], w=[c.r_act[jc]])
        for i in range(KC if DBG.get('p2', 1) else 0):
            ds_ = dslot % 3
            dslot += 1
            r0 = fs * FPS * 128
            P.add("pool", lambda e, ds_=ds_, i=i, r0=r0: e.dma_start(
                out=c.wd[ds_][:],
                in_=wd[r0:r0 + FPS * 128, i * 128:(i + 1) * 128].rearrange("(jc p) d -> p jc d", p=128)),
                w=[c.r_wd[ds_]], dma=True)
            for b in range(2):
                bk = bank % 8
                bank += 1
                for jc in range(FPS):
                    P.add("pe", lambda e, ds_=ds_, jc=jc, b=b, bk=bk: e.matmul(
                        out=c.ps[bk][:], lhsT=c.wd[ds_][:, jc, :], rhs=c.act[:, jc, b * 512:(b + 1) * 512],
                        start=(jc == 0), stop=(jc == FPS - 1)),
                        r=[c.r_wd[ds_], c.r_act[jc]], w=[c.r_ps[bk]])
                if fs == 0:
                    P.add("act", lambda e, i=i, b=b, bk=bk: e.copy(
                        out=c.big[:, i, b * 512:(b + 1) * 512], in_=c.ps[bk][:]),
                        r=[c.r_ps[bk]], w=[c.r_big[i]])
                else:
                    P.add("dve", lambda e, i=i, b=b, bk=bk: e.tensor_tensor(
                        out=c.big[:, i, b * 512:(b + 1) * 512], in0=c.big[:, i, b * 512:(b + 1) * 512],
                        in1=c.ps[bk][:], op=ALU.add),
                        r=[c.r_ps[bk], c.r_big[i]], w=[c.r_big[i]])
    if DBG.get('post', 1):
        rms_stats(P, c, lambda i: c.big[:, i, :], c.r_big, KC, D, c.rstd, c.r_rstd)
    else:
        for i in range(KC):
            P.add('sp', lambda e, i=i: e.dma_start(out=xout_v[:, i, :], in_=c.big[:, i, :]), r=[c.r_big[i]], dma=True, final=final)
    for i in range(KC if DBG.get('post', 1) else 0):
        xs = i % len(c.xb)
        ys = i % 2
        P.add("sp", lambda e, i=i, xs=xs: e.dma_start(out=c.xb[xs][:], in_=xin_v[:, i, :]),
              w=[c.r_xb[xs]], dma=True)
        P.add("dve", lambda e, i=i, ys=ys: e.scalar_tensor_tensor(
            out=c.yb[ys][:], in0=c.big[:, i, :], scalar=c.ghalf[:, i:i + 1], in1=c.rstd[:],
            op0=ALU.mult, op1=ALU.mult),
            r=[c.r_big[i], c.r_ghalf, c.r_rstd], w=[c.r_yb[ys]])
        P.add("dve", lambda e, xs=xs, ys=ys: e.tensor_tensor(
            out=c.yb[ys][:], in0=c.yb[ys][:], in1=c.xb[xs][:], op=ALU.add),
            r=[c.r_yb[ys], c.r_xb[xs]], w=[c.r_yb[ys]])
        P.add("sp", lambda e, i=i, ys=ys: e.dma_start(out=xout_v[:, i, :], in_=c.yb[ys][:]),
              r=[c.r_yb[ys]], dma=True, final=final)


def build_ffn_launch():
    nc = bass.Bass("TRN2", target_bir_lowering=False)
    x_in = nc.dram_tensor("x_in", [D, TOK], F32, kind="ExternalInput").ap()
    wg = nc.dram_tensor("wg", [D, DFF], F32, kind="ExternalInput").ap()
    wu = nc.dram_tensor("wu", [D, DFF], F32, kind="ExternalInput").ap()
    wd = nc.dram_tensor("wd", [DFF, D], F32, kind="ExternalInput").ap()
    gains = nc.dram_tensor("gains", [128, 64], F32, kind="ExternalInput").ap()
    x_out = nc.dram_tensor("x_out", [D, TOK], F32, kind="ExternalOutput").ap()
    with ExitStack() as es:
        c = make_ctx(nc, es)
        alloc_ffn(c)
        P = Prog(nc, es)
        load_consts(P, c)
        P.add("sp", lambda e: e.dma_start(out=c.gains[:, 0:64], in_=gains), w=[c.r_gains], dma=True)
        ffn_phase(P, c, x_in, x_out, wg, wu, wd, 0, KC, final=True)
        P.finish()
        P.emit()
    return nc


def tok_index(core):
    a = np.arange(512 * core, 512 * core + 512)
    b = np.arange(512 * (15 - core), 512 * (15 - core) + 512)
    return np.concatenate([a, b])


def pack_gain(g):
    return np.ascontiguousarray(g.reshape(KC, 128).T)


PI = float(np.pi)
NH = 8
W_IN_COLS = 4160
G_MIXPRE, G_QN, G_KVN, G_ROPE, G_MIXPOST = 0, 16, 20, 32, 40


def alloc_proj(c):
    sb = c.sb
    c.wp = [sb("wp%d" % i, [128, KC, 128], BF16) for i in range(3)]
    c.r_wp = [Res("wp%d" % i) for i in range(3)]
    c.wv = [sb("wv%d" % i, [128, KC, 512], BF16) for i in range(1)]
    c.r_wv = [Res("wv%d" % i) for i in range(1)]
    c.st = [sb("st%d" % i, [128, TOK], BF16) for i in range(3)]
    c.r_st = [Res("st%d" % i) for i in range(3)]
    c.t1 = [sb("t1_%d" % i, [64, 512], F32) for i in range(2)]
    c.r_t1 = [Res("t1_%d" % i) for i in range(2)]
    c.t2 = [sb("t2_%d" % i, [64, 512], F32) for i in range(2)]
    c.r_t2 = [Res("t2_%d" % i) for i in range(2)]
    c.lat = sb("lat", [128, 8, TOK], BF16)
    c.r_lat = [Res("lat%d" % i) for i in range(8)]
    c.vst = [sb("vst%d" % i, [128, 1024], BF16) for i in range(2)]
    c.r_vst = [Res("vst%d" % i) for i in range(2)]


def alloc_rope(c):
    sb = c.sb
    c.posi = sb("posi", [128, TOK], I32)
    c.ta = sb("ta", [64, TOK], F32)
    c.tb = sb("tb", [64, TOK], F32)
    c.tk = sb("tk", [64, TOK], I32)
    c.r_ta, c.r_tb, c.r_tk = Res("ta"), Res("tb"), Res("tk")


def rope_tables(P, c, pos_rep):
    r_pos = Res("posi")
    P.add("sp", lambda e: e.dma_start(out=c.posi[:], in_=pos_rep), w=[r_pos], dma=True)
    for nm, nrows, col in (("A", 64, G_ROPEC), ("P", 32, G_ROPEC + 2)):
        P.add("dve", lambda e, nrows=nrows: e.tensor_copy(out=c.ta[0:nrows, :], in_=c.posi[0:nrows, :]),
              r=[r_pos], w=[c.r_ta])
        P.add("dve", lambda e, nrows=nrows, col=col: e.tensor_scalar(
            out=c.ta[0:nrows, :], in0=c.ta[0:nrows, :], scalar1=c.gains[0:nrows, col:col + 1], scalar2=None,
            op0=ALU.mult), r=[c.r_ta, c.r_gains], w=[c.r_ta])
        for kind, shift in (("sin", 0.0), ("cos", PI / 2)):
            tab = c.tabs[kind + nm]
            rt = c.r_tabs[kind + nm]
            P.add("dve", lambda e, nrows=nrows, shift=shift: e.tensor_scalar(
                out=c.tb[0:nrows, :], in0=c.ta[0:nrows, :], scalar1=shift, scalar2=None, op0=ALU.add),
                r=[c.r_ta], w=[c.r_tb])
            P.add("dve", lambda e, nrows=nrows, tab=tab: e.tensor_scalar(
                out=tab[0:nrows, :], in0=c.tb[0:nrows, :], scalar1=1.0 / (2 * PI), scalar2=None, op0=ALU.mult),
                r=[c.r_tb], w=[rt])
            P.add("dve", lambda e, nrows=nrows, tab=tab: e.tensor_copy(out=c.tk[0:nrows, :], in_=tab[0:nrows, :]),
                  r=[rt], w=[c.r_tk])
            P.add("dve", lambda e, nrows=nrows, tab=tab: e.tensor_copy(out=tab[0:nrows, :], in_=c.tk[0:nrows, :]),
                  r=[c.r_tk], w=[rt])
            P.add("dve", lambda e, nrows=nrows, tab=tab: e.scalar_tensor_tensor(
                out=c.tb[0:nrows, :], in0=tab[0:nrows, :], scalar=-2 * PI, in1=c.tb[0:nrows, :],
                op0=ALU.mult, op1=ALU.add), r=[rt, c.r_tb], w=[c.r_tb])
            P.add("dve", lambda e, nrows=nrows, tab=tab: e.tensor_scalar(
                out=tab[0:nrows, :], in0=c.tb[0:nrows, :], scalar1=PI, scalar2=-2 * PI, op0=ALU.is_gt, op1=ALU.mult),
                r=[c.r_tb], w=[rt])
            P.add("dve", lambda e, nrows=nrows, tab=tab: e.tensor_tensor(
                out=c.tb[0:nrows, :], in0=c.tb[0:nrows, :], in1=tab[0:nrows, :], op=ALU.add),
                r=[rt, c.r_tb], w=[c.r_tb])
            P.add("dve", lambda e, nrows=nrows, tab=tab: e.tensor_scalar(
                out=tab[0:nrows, :], in0=c.tb[0:nrows, :], scalar1=-PI, scalar2=2 * PI, op0=ALU.is_lt, op1=ALU.mult),
                r=[c.r_tb], w=[rt])
            P.add("dve", lambda e, nrows=nrows, tab=tab: e.tensor_tensor(
                out=c.tb[0:nrows, :], in0=c.tb[0:nrows, :], in1=tab[0:nrows, :], op=ALU.add),
                r=[rt, c.r_tb], w=[c.r_tb])
            P.add("act", lambda e, nrows=nrows, tab=tab: e.activation(out=tab[0:nrows, :], in_=c.tb[0:nrows, :], func=AF.Sin),
                  r=[c.r_tb], w=[rt])
            if kind == "sin":
                P.add("dve", lambda e, nrows=nrows, tab=tab, col=col: e.tensor_scalar(
                    out=tab[0:nrows, :], in0=tab[0:nrows, :], scalar1=c.gains[0:nrows, col + 1:col + 2], scalar2=None,
                    op0=ALU.mult), r=[rt, c.r_gains], w=[rt])


class Rot:
    def __init__(self):
        self.w = 0
        self.bank = 0
        self.st = 0
        self.t = 0
        self.v = 0
        self.vs = 0


def lin_fm(P, c, rot, wsrc, c0, M, nk, rhs, r_rhs, evac):
    ws = rot.w % 3
    rot.w += 1
    P.add("pool", lambda e: e.dma_start(
        out=c.wp[ws][:, 0:nk, 0:M], in_=wsrc[:, c0:c0 + M].rearrange("(kc kp) f -> kp kc f", kp=128)),
        w=[c.r_wp[ws]], dma=True)
    banks = []
    for b in range(2):
        bk = rot.bank % 8
        rot.bank += 1
        banks.append(bk)
    for k in range(nk):
        for b in range(2):
            P.add("pe", lambda e, k=k, b=b: e.matmul(
                out=c.ps[banks[b]][0:M, :], lhsT=c.wp[ws][:, k, 0:M], rhs=rhs[:, k, b * 512:(b + 1) * 512],
                start=(k == 0), stop=(k == nk - 1)),
                r=[c.r_wp[ws], r_rhs[k]], w=[c.r_ps[banks[b]]])
    for b in range(2):
        evac(b, banks[b])
    return banks


def proj_phase(P, c, x_in, w_in, w_sw, w_uq, w_uq_sw, w_ukv, w_vd, w_va, outs):
    rot = Rot()
    xin_v = x_in.rearrange("(kc kp) t -> kp kc t", kp=128)
    for q in range(4):
        P.add("sp", lambda e, q=q: e.dma_start(out=c.big[:, 4 * q:4 * q + 4, :], in_=xin_v[:, 4 * q:4 * q + 4, :]),
              w=c.r_big[4 * q:4 * q + 4], dma=True)
    rms_stats(P, c, lambda i: c.big[:, i, :], c.r_big, KC, D, c.rstd, c.r_rstd)
    for i in range(KC):
        P.add("dve", lambda e, i=i: e.scalar_tensor_tensor(
            out=c.h[:, i, :], in0=c.big[:, i, :], scalar=c.gains[:, c.gb + G_MIXPRE + i:c.gb + G_MIXPRE + i + 1],
            in1=c.rstd[:], op0=ALU.mult, op1=ALU.mult),
            r=[c.r_big[i], c.r_rstd, c.r_gains], w=[c.r_h[i]])

    def store(dst, rows, st):
        P.add("sp", lambda e: e.dma_start(out=dst, in_=c.st[st][0:rows, :]), r=[c.r_st[st]], dma=True)

    def evac_plain(dst, rows):
        st = rot.st % 3
        rot.st += 1

        def ev(b, bk):
            P.add("act", lambda e: e.copy(out=c.st[st][0:rows, b * 512:(b + 1) * 512], in_=c.ps[bk][0:rows, :]),
                  r=[c.r_ps[bk]], w=[c.r_st[st]])
            if b == 1:
                store(dst, rows, st)
        return ev

    def rope_chunk(wmain, cm, M, wswp, cs, R, nk, rhs, r_rhs, tabn, dst):
        st = rot.st % 3
        rot.st += 1
        held = {}

        def ev_main(b, bk):
            held[b] = bk
        lin_fm(P, c, rot, wmain, cm, M, nk, rhs, r_rhs, ev_main)

        def ev_sw(b, bk2):
            bk = held[b]
            t = rot.t % 2
            rot.t += 1
            cs_, sn_ = c.tabs["cos" + tabn], c.tabs["sin" + tabn]
            P.add("dve", lambda e: e.tensor_tensor(
                out=c.t1[t][0:R, :], in0=c.ps[bk][0:R, :], in1=cs_[0:R, b * 512:(b + 1) * 512], op=ALU.mult),
                r=[c.r_ps[bk], c.r_tabs["cos" + tabn]], w=[c.r_t1[t]])
            P.add("dve", lambda e: e.tensor_tensor(
                out=c.t2[t][0:R, :], in0=c.ps[bk2][0:R, :], in1=sn_[0:R, b * 512:(b + 1) * 512], op=ALU.mult),
                r=[c.r_ps[bk2], c.r_tabs["sin" + tabn]], w=[c.r_t2[t]])
            if M > R:
                P.add("act", lambda e: e.copy(out=c.st[st][0:M, b * 512:(b + 1) * 512], in_=c.ps[bk][0:M, :]),
                      r=[c.r_ps[bk]], w=[c.r_st[st]])
            P.add("dve", lambda e: e.tensor_tensor(
                out=c.st[st][0:R, b * 512:(b + 1) * 512], in0=c.t1[t][0:R, :], in1=c.t2[t][0:R, :], op=ALU.add),
                r=[c.r_t1[t], c.r_t2[t]], w=[c.r_st[st]])
            if b == 1:
                store(dst, M, st)
        lin_fm(P, c, rot, wswp, cs, R, nk, rhs, r_rhs, ev_sw)

    for j in range(8):
        def ev(b, bk, j=j):
            P.add("act", lambda e: e.copy(out=c.big[:, j, b * 512:(b + 1) * 512], in_=c.ps[bk][:]),
                  r=[c.r_ps[bk]] + c.r_h, w=[c.r_big[j]])
        lin_fm(P, c, rot, w_in, j * 128, 128, KC, c.h, c.r_h, ev)
    for grp, gcol in ((0, c.gb + G_QN), (1, c.gb + G_KVN)):
        rms_stats(P, c, lambda i, grp=grp: c.big[:, 4 * grp + i, :], c.r_big[4 * grp:4 * grp + 4], 4, 512,
                  c.rstd, c.r_rstd, ps_banks=(2 * grp, 2 * grp + 1))
        for i in range(4):
            P.add("dve", lambda e, i=i, grp=grp, gcol=gcol: e.scalar_tensor_tensor(
                out=c.lat[:, 4 * grp + i, :], in0=c.big[:, 4 * grp + i, :], scalar=c.gains[:, gcol + i:gcol + i + 1],
                in1=c.rstd[:], op0=ALU.mult, op1=ALU.mult),
                r=[c.r_big[4 * grp + i], c.r_rstd, c.r_gains], w=[c.r_lat[4 * grp + i]])
    latq, r_latq = c.lat[:, 0:4, :], c.r_lat[0:4]
    latkv, r_latkv = c.lat[:, 4:8, :], c.r_lat[4:8]
    rope_chunk(w_in, 1024, 64, w_sw, 0, 64, KC, c.h, c.r_h, "A", outs["kr"])
    for hd in range(NH):
        rope_chunk(w_in, 1088 + hd * 128, 128, w_sw, 64 + hd * 32, 32, KC, c.h, c.r_h, "P", outs["qd"][hd])
        rope_chunk(w_in, 2112 + hd * 128, 128, w_sw, 64 + 256 + hd * 32, 32, KC, c.h, c.r_h, "P", outs["kd"][hd])
    for hd in range(NH):
        lin_fm(P, c, rot, w_uq, hd * 192, 128, 4, latq, r_latq, evac_plain(outs["qn"][hd], 128))
        rope_chunk(w_uq, hd * 192 + 128, 64, w_uq_sw, hd * 64, 64, 4, latq, r_latq, "A", outs["qr"][hd])
        lin_fm(P, c, rot, w_ukv, hd * 256, 128, 4, latkv, r_latkv, evac_plain(outs["kn"][hd], 128))
    for (wsrc, nk, lhs, r_lhs, dst) in ((w_vd, KC, c.h, c.r_h, outs["vd"]), (w_va, 4, latkv, r_latkv, outs["va"])):
        for half in range(2):
            v = 0
            P.add("pool", lambda e, v=v, wsrc=wsrc, nk=nk, half=half: e.dma_start(
                out=c.wv[v][:, 0:nk, :],
                in_=wsrc[:, half * 512:(half + 1) * 512].rearrange("(kc kp) f -> kp kc f", kp=128)),
                w=[c.r_wv[v]], dma=True)
            for tt in range(TOK // 128):
                bk = rot.bank % 8
                rot.bank += 1
                for k in range(nk):
                    P.add("pe", lambda e, k=k, tt=tt, bk=bk, v=v, lhs=lhs, nk=nk: e.matmul(
                        out=c.ps[bk][:], lhsT=lhs[:, k, tt * 128:(tt + 1) * 128], rhs=c.wv[v][:, k, :],
                        start=(k == 0), stop=(k == nk - 1)),
                        r=[c.r_wv[v], r_lhs[k]], w=[c.r_ps[bk]])
                vs = rot.vs % 2
                rot.vs += 1
                P.add("act", lambda e, bk=bk, vs=vs: e.copy(out=c.vst[vs][:, 0:512], in_=c.ps[bk][:]),
                      r=[c.r_ps[bk]], w=[c.r_vst[vs]])
                P.add("sp", lambda e, vs=vs, tt=tt, half=half, dst=dst: e.dma_start(
                    out=dst[tt * 128:(tt + 1) * 128, half * 512:(half + 1) * 512], in_=c.vst[vs][:, 0:512]),
                    r=[c.r_vst[vs]], dma=True)


def rope_consts():
    g = np.zeros((128, 8), np.float32)
    inv_a = (500000.0 ** (-np.arange(0, 64, 2, dtype=np.float32) / 64)).astype(np.float32)
    inv_p = (500000.0 ** (-np.arange(0, 32, 2, dtype=np.float32) / 32)).astype(np.float32)
    g[0:64, 0] = np.concatenate([inv_a, inv_a])
    g[0:64, 1] = np.concatenate([-np.ones(32), np.ones(32)])
    g[0:32, 2] = np.concatenate([inv_p, inv_p])
    g[0:32, 3] = np.concatenate([-np.ones(16), np.ones(16)])
    return g


def proj_weights(w_in, w_uq, w_ukv):
    sw = [w_in[:, 1024 + 32:1024 + 64], w_in[:, 1024:1024 + 32]]
    for base in (1088, 2112):
        for hd in range(NH):
            c0 = base + hd * 128
            sw += [w_in[:, c0 + 16:c0 + 32], w_in[:, c0:c0 + 16]]
    w_sw = np.ascontiguousarray(np.concatenate(sw, axis=1))
    uqs = []
    for hd in range(NH):
        c0 = hd * 192 + 128
        uqs += [w_uq[:, c0 + 32:c0 + 64], w_uq[:, c0:c0 + 32]]
    w_uq_sw = np.ascontiguousarray(np.concatenate(uqs, axis=1))
    w_vd = np.ascontiguousarray(w_in[:, 3136:4160])
    w_va = np.ascontiguousarray(np.concatenate([w_ukv[:, hd * 256 + 128:hd * 256 + 256] for hd in range(NH)], axis=1))
    return w_sw, w_uq_sw, w_vd, w_va


KT_ROWS = 2112


def alloc_attn(c):
    sb = c.sb
    c.kn = sb("kn", [128, NCORE, TOK], BF16)
    c.kr = sb("kr", [64, NCORE, TOK], BF16)
    c.va = sb("va", [128, 64, 128], BF16)
    c.r_kn, c.r_kr, c.r_va = Res("kn"), Res("kr"), Res("va")
    c.q1 = sb("q1", [128, TOK], BF16)
    c.q2 = sb("q2", [64, TOK], BF16)
    c.r_q1, c.r_q2 = Res("q1"), Res("q2")
    c.pt = [sb("pt%d" % i, [128, 512], BF16) for i in range(3)]
    c.r_pt = [Res("pt%d" % i) for i in range(3)]
    c.pm = [sb("pm%d" % i, [128, 512], BF16) for i in range(3)]
    c.r_pm = [Res("pm%d" % i) for i in range(3)]
    c.dm = [sb("dm%d" % i, [128, 512], BF16) for i in range(3)]
    c.r_dm = [Res("dm%d" % i) for i in range(3)]
    c.qidx = sb("qidx", [128, TOK], F32)
    c.kidx = sb("kidx", [128, 64], F32)
    c.r_idx = Res("idx")
    c.rden = sb("rden", [128, 512], F32)
    c.r_rden = Res("rden")


def alloc_attn_out(c):
    sb = c.sb
    c.wp = [sb("wp%d" % i, [128, KC, 128], BF16) for i in range(2)]
    c.r_wp = [Res("wp%d" % i) for i in range(2)]
    c.xb = [sb("xb%d" % i, [128, TOK], F32) for i in range(2)]
    c.r_xb = [Res("xb%d" % i) for i in range(2)]
    c.yb = [sb("yb%d" % i, [128, TOK], F32) for i in range(2)]
    c.r_yb = [Res("yb%d" % i) for i in range(2)]


def key_tiles(b):
    return [(r, lt) for r in range(NCORE) for lt in range(4 if b == 0 else 8)]


def attn_phase(P, c, G):
    KTv = G["KT_g"].rearrange("(r row) t -> row r t", r=NCORE)
    Vg = G["V_g"]
    P.add("sp", lambda e: e.dma_start(out=c.qidx[:], in_=G["qidx"]), w=[c.r_idx], dma=True)
    P.add("sp", lambda e: e.dma_start(out=c.kidx[:], in_=G["kidx"]), w=[c.r_idx], dma=True)
    P.add("sp", lambda e: e.dma_start(out=c.kr[:], in_=KTv[2048:2112]), w=[c.r_kr], dma=True)
    st = {"sb": 0, "p": 0, "acc": 0, "dm": 0}
    SB = (0, 1, 6, 7)

    def core(b, score_mm, scale, mask_op, out_chunk):
        tl = key_tiles(b)
        ob, db = (2, 3) if st["acc"] % 2 == 0 else (4, 5)
        st["acc"] += 1
        for ti, (r, lt) in enumerate(tl):
            sbk = SB[st["sb"] % 4]
            st["sb"] += 1
            ps_ = st["p"] % 3
            st["p"] += 1
            score_mm(r, lt, sbk)
            P.add("act", lambda e, sbk=sbk, ps_=ps_: e.activation(
                out=c.pt[ps_][:], in_=c.ps[sbk][:], func=AF.Exp, scale=scale),
                r=[c.r_ps[sbk]], w=[c.r_pt[ps_]])
            mask_op(r * 8 + lt, ps_)
            first, last = (ti == 0), (ti == len(tl) - 1)
            P.add("pe", lambda e, ps_=ps_, k=r * 8 + lt, first=first, last=last: e.matmul(
                out=c.ps[ob][:], lhsT=c.va[:, k, :], rhs=c.pm[ps_][:], start=first, stop=last),
                r=[c.r_va, c.r_pm[ps_]], w=[c.r_ps[ob]])
            P.add("pe", lambda e, ps_=ps_, first=first, last=last: e.matmul(
                out=c.ps[db][:], lhsT=c.ones[:], rhs=c.pm[ps_][:], start=first, stop=last),
                r=[c.r_ones, c.r_pm[ps_]], w=[c.r_ps[db]])
        P.add("dve", lambda e: e.reciprocal(out=c.rden[:], in_=c.ps[db][:]), r=[c.r_ps[db]], w=[c.r_rden])
        P.add("dve", lambda e: e.tensor_tensor(
            out=c.h[:, out_chunk, b * 512:(b + 1) * 512], in0=c.ps[ob][:], in1=c.rden[:], op=ALU.mult),
            r=[c.r_ps[ob], c.r_rden], w=[c.r_h[out_chunk]])

    for mixer in range(2):
        for hd in range(NH):
            krow = (0 if mixer == 0 else 1024) + hd * 128
            vcol = (0 if mixer == 0 else 1024) + hd * 128
            P.add("sp", lambda e, krow=krow: e.dma_start(out=c.kn[:], in_=KTv[krow:krow + 128]), w=[c.r_kn], dma=True)
            P.add("sp", lambda e, vcol=vcol: e.dma_start(
                out=c.va[:], in_=Vg[:, vcol:vcol + 128].rearrange("(k p) v -> p k v", p=128)), w=[c.r_va], dma=True)
            qsrc = G["qn"][hd] if mixer == 0 else G["qd"][hd]
            P.add("sp", lambda e, qsrc=qsrc: e.dma_start(out=c.q1[:], in_=qsrc), w=[c.r_q1], dma=True)
            if mixer == 0:
                P.add("sp", lambda e, hd=hd: e.dma_start(out=c.q2[:], in_=G["qr"][hd]), w=[c.r_q2], dma=True)
            for b in range(2):
                if mixer == 0:
                    def score_mm(r, lt, sbk, b=b):
                        P.add("pe", lambda e: e.matmul(
                            out=c.ps[sbk][:], lhsT=c.kn[:, r, lt * 128:(lt + 1) * 128],
                            rhs=c.q1[:, b * 512:(b + 1) * 512], start=True, stop=False),
                            r=[c.r_kn, c.r_q1], w=[c.r_ps[sbk]])
                        P.add("pe", lambda e: e.matmul(
                            out=c.ps[sbk][:], lhsT=c.kr[0:64, r, lt * 128:(lt + 1) * 128],
                            rhs=c.q2[0:64, b * 512:(b + 1) * 512], start=False, stop=True),
                            r=[c.r_kr, c.r_q2], w=[c.r_ps[sbk]])

                    def mask_op(k, ps_, b=b):
                        P.add("dve", lambda e: e.scalar_tensor_tensor(
                            out=c.pm[ps_][:], in0=c.qidx[:, b * 512:(b + 1) * 512], scalar=c.kidx[:, k:k + 1],
                            in1=c.pt[ps_][:], op0=ALU.is_ge, op1=ALU.mult),
                            r=[c.r_idx, c.r_pt[ps_]], w=[c.r_pm[ps_]])
                    core(b, score_mm, 1.0 / float(np.sqrt(192.0)), mask_op, hd)
                else:
                    def score_mm(r, lt, sbk, b=b):
                        P.add("pe", lambda e: e.matmul(
                            out=c.ps[sbk][:], lhsT=c.kn[:, r, lt * 128:(lt + 1) * 128],
                            rhs=c.q1[:, b * 512:(b + 1) * 512], start=True, stop=True),
                            r=[c.r_kn, c.r_q1], w=[c.r_ps[sbk]])

                    def mask_op(k, ps_, b=b):
                        ds_ = st["dm"] % 3
                        st["dm"] += 1
                        P.add("pool", lambda e: e.dma_start(out=c.dm[ds_][:], in_=G["dmask"][b, k]),
                              w=[c.r_dm[ds_]], dma=True)
                        P.add("dve", lambda e: e.tensor_tensor(
                            out=c.pm[ps_][:], in0=c.pt[ps_][:], in1=c.dm[ds_][:], op=ALU.mult),
                            r=[c.r_dm[ds_], c.r_pt[ps_]], w=[c.r_pm[ps_]])
                    core(b, score_mm, 1.0 / float(np.sqrt(128.0)), mask_op, NH + hd)


def attn_out_phase(P, c, x_in, x_out, w_o, final=False):
    xin_v = x_in.rearrange("(kc kp) t -> kp kc t", kp=128)
    xout_v = x_out.rearrange("(kc kp) t -> kp kc t", kp=128)
    rot = Rot()
    for i in range(KC):
        ws = rot.w % 2
        rot.w += 1
        P.add("pool", lambda e, ws=ws, i=i: e.dma_start(
            out=c.wp[ws][:], in_=w_o[:, i * 128:(i + 1) * 128].rearrange("(kc kp) f -> kp kc f", kp=128)),
            w=[c.r_wp[ws]], dma=True)
        for b in range(2):
            bk = rot.bank % 8
            rot.bank += 1
            for k in range(KC):
                P.add("pe", lambda e, ws=ws, k=k, b=b, bk=bk: e.matmul(
                    out=c.ps[bk][:], lhsT=c.wp[ws][:, k, :], rhs=c.h[:, k, b * 512:(b + 1) * 512],
                    start=(k == 0), stop=(k == KC - 1)),
                    r=[c.r_wp[ws], c.r_h[k]], w=[c.r_ps[bk]])
            P.add("act", lambda e, i=i, b=b, bk=bk: e.copy(out=c.big[:, i, b * 512:(b + 1) * 512], in_=c.ps[bk][:]),
                  r=[c.r_ps[bk]], w=[c.r_big[i]])
    rms_stats(P, c, lambda i: c.big[:, i, :], c.r_big, KC, D, c.rstd, c.r_rstd)
    gp = c.gb + G_MIXPOST
    for i in range(KC):
        xs = i % 2
        ys = i % 2
        P.add("sp", lambda e, i=i, xs=xs: e.dma_start(out=c.xb[xs][:], in_=xin_v[:, i, :]), w=[c.r_xb[xs]], dma=True)
        P.add("dve", lambda e, i=i, ys=ys: e.scalar_tensor_tensor(
            out=c.yb[ys][:], in0=c.big[:, i, :], scalar=c.gains[:, gp + i:gp + i + 1], in1=c.rstd[:],
            op0=ALU.mult, op1=ALU.mult),
            r=[c.r_big[i], c.r_gains, c.r_rstd], w=[c.r_yb[ys]])
        P.add("dve", lambda e, xs=xs, ys=ys: e.tensor_tensor(
            out=c.yb[ys][:], in0=c.yb[ys][:], in1=c.xb[xs][:], op=ALU.add),
            r=[c.r_yb[ys], c.r_xb[xs]], w=[c.r_yb[ys]])
        P.add("sp", lambda e, i=i, ys=ys: e.dma_start(out=xout_v[:, i, :], in_=c.yb[ys][:]),
              r=[c.r_yb[ys]], dma=True, final=final)


def attn_tables(core):
    tok = tok_index(core)
    qidx = np.ascontiguousarray(np.broadcast_to(tok.astype(np.float32), (128, TOK)))
    kpos = np.stack([tok_index(r) for r in range(NCORE)]).reshape(64, 128)
    kidx = np.ascontiguousarray(kpos.T.astype(np.float32))
    dmask = np.zeros((2, 64, 128, 512), np.float32)
    for b in range(2):
        q = tok[b * 512:(b + 1) * 512]
        for k in range(64):
            dl = q[None, :] - kpos[k][:, None]
            ok = dl >= 0
            dmask[b, k] = ((ok & (dl <= 128)).astype(np.float32) + (ok & (dl % 4 == 0) & (dl <= 512))
                           + (ok & (dl % 16 == 0) & (dl <= 2048)))
    return qidx, kidx, dmask


W_SPECS = {
    "ffn1_w_gate": [DEPTH, D, DFF], "ffn1_w_up": [DEPTH, D, DFF], "ffn1_w_down": [DEPTH, DFF, D],
    "ffn2_w_gate": [DEPTH, D, DFF], "ffn2_w_up": [DEPTH, D, DFF], "ffn2_w_down": [DEPTH, DFF, D],
    "w_in": [DEPTH, D, W_IN_COLS], "w_sw": [DEPTH, D, 576], "mla_w_uq": [DEPTH, 512, 1536],
    "w_uq_sw": [DEPTH, 512, 512], "mla_w_ukv": [DEPTH, 512, 2048], "w_vd": [DEPTH, D, 1024],
    "w_va": [DEPTH, 512, 1024], "w_o": [DEPTH, D, D],
}


def build_fused(depth=DEPTH):
    nc = bass.Bass("TRN2", target_bir_lowering=False)
    x_in = nc.dram_tensor("x_in", [D, TOK], F32, kind="ExternalInput").ap()
    pos = nc.dram_tensor("pos", [128, TOK], I32, kind="ExternalInput").ap()
    gains = nc.dram_tensor("gains", [128, GW], F32, kind="ExternalInput").ap()
    W = {k: nc.dram_tensor(k, shp, F32, kind="ExternalInput").ap() for k, shp in W_SPECS.items()}
    qidx = nc.dram_tensor("qidx", [128, TOK], F32, kind="ExternalInput").ap()
    kidx = nc.dram_tensor("kidx", [128, 64], F32, kind="ExternalInput").ap()
    dmask = nc.dram_tensor("dmask", [2, 64, 128, 512], BF16, kind="ExternalInput").ap()
    x_out = nc.dram_tensor("x_out", [D, TOK], F32, kind="ExternalOutput").ap()
    xs = [nc.dram_tensor("xs%d" % i, [D, TOK], F32, kind="Internal").ap() for i in range(2)]
    qn = nc.dram_tensor("q_n", [NH, 128, TOK], BF16, kind="Internal").ap()
    qr = nc.dram_tensor("q_r", [NH, 64, TOK], BF16, kind="Internal").ap()
    qd = nc.dram_tensor("q_d", [NH, 128, TOK], BF16, kind="Internal").ap()
    KT_s = [nc.dram_tensor("KT_s%d" % i, [KT_ROWS, TOK], BF16, kind="Internal").ap() for i in range(2)]
    V_s = [nc.dram_tensor("V_s%d" % i, [TOK, 2048], BF16, kind="Internal").ap() for i in range(2)]
    KT_g = [nc.dram_tensor("KT_g%d" % i, [NCORE * KT_ROWS, TOK], BF16, kind="Internal").ap() for i in range(2)]
    V_g = [nc.dram_tensor("V_g%d" % i, [NCORE * TOK, 2048], BF16, kind="Internal").ap() for i in range(2)]
    groups = [list(range(NCORE))]
    with ExitStack() as es:
        c = make_ctx(nc, es)
        P = Prog(nc, es)
        load_consts(P, c)
        P.add("sp", lambda e: e.dma_start(out=c.gains[:], in_=gains), w=[c.r_gains], dma=True)
        with ExitStack() as pes:
            c.es = pes
            alloc_rope(c)
            rope_tables(P, c, pos)
            P.barrier()
            P.emit()
        cur = x_in
        nxt = 0

        def ffn(cur, dst, l, which, gb, final=False):
            with ExitStack() as pes:
                c.es = pes
                alloc_ffn(c)
                ffn_phase(P, c, cur, dst, W[which + "_w_gate"][l], W[which + "_w_up"][l], W[which + "_w_down"][l],
                          gb, gb + KC, final=final)
                if final:
                    P.finish()
                P.barrier()
                P.emit()

        for l in range(depth):
            ffn(cur, xs[nxt], l, "ffn1", 64 * (3 * l))
            cur, nxt = xs[nxt], 1 - nxt
            par = l % 2
            outs = {"qn": [qn[h] for h in range(NH)], "qr": [qr[h] for h in range(NH)], "qd": [qd[h] for h in range(NH)],
                    "kn": [KT_s[par][h * 128:(h + 1) * 128, :] for h in range(NH)],
                    "kd": [KT_s[par][1024 + h * 128:1024 + (h + 1) * 128, :] for h in range(NH)],
                    "kr": KT_s[par][2048:2112, :], "va": V_s[par][:, 0:1024], "vd": V_s[par][:, 1024:2048]}
            with ExitStack() as pes:
                c.es = pes
                c.gb = 64 * (3 * l + 1)
                alloc_proj(c)
                proj_phase(P, c, cur, W["w_in"][l], W["w_sw"][l], W["mla_w_uq"][l], W["w_uq_sw"][l],
                           W["mla_w_ukv"][l], W["w_vd"][l], W["w_va"][l], outs)
                P.barrier()
                P.add("pool", lambda e, par=par: e.collective_compute(
                    "AllGather", ALU.bypass, replica_groups=groups, ins=[KT_s[par]], outs=[KT_g[par]]), cc=True)
                P.add("pool", lambda e, par=par: e.collective_compute(
                    "AllGather", ALU.bypass, replica_groups=groups, ins=[V_s[par]], outs=[V_g[par]]), cc=True)
                P.barrier()
                P.emit()
            with ExitStack() as pes:
                c.es = pes
                alloc_attn(c)
                G = {"KT_g": KT_g[par], "V_g": V_g[par], "qn": outs["qn"], "qr": outs["qr"], "qd": outs["qd"],
                     "qidx": qidx, "kidx": kidx, "dmask": dmask}
                attn_phase(P, c, G)
                P.barrier()
                P.emit()
            with ExitStack() as pes:
                c.es = pes
                alloc_attn_out(c)
                attn_out_phase(P, c, cur, xs[nxt], W["w_o"][l])
                P.barrier()
                P.emit()
            cur, nxt = xs[nxt], 1 - nxt
            last = (l == depth - 1)
            ffn(cur, x_out if last else xs[nxt], l, "ffn2", 64 * (3 * l + 2), final=last)
            if not last:
                cur, nxt = xs[nxt], 1 - nxt
    return nc


_PROGS = {}


def kernel(x, positions, ffn1_pre_g, ffn1_post_g, ffn1_w_gate, ffn1_w_up, ffn1_w_down,
           mix_pre_g, mix_post_g, w_in, mla_q_norm_g, mla_w_uq, mla_kv_norm_g, mla_w_ukv,
           w_o, ffn2_pre_g, ffn2_post_g, ffn2_w_gate, ffn2_w_up, ffn2_w_down):
    f32 = lambda a: np.ascontiguousarray(np.asarray(a), dtype=np.float32)
    x = f32(x)
    positions = np.asarray(positions)
    toks = [tok_index(c) for c in range(NCORE)]
    gains = np.zeros((128, GW), np.float32)
    for l in range(DEPTH):
        b = 64 * 3 * l
        gains[:, b:b + KC] = pack_gain(f32(ffn1_pre_g[l]))
        gains[:, b + KC:b + 2 * KC] = pack_gain(f32(ffn1_post_g[l]))
        b += 64
        gains[:, b + G_MIXPRE:b + G_MIXPRE + KC] = pack_gain(f32(mix_pre_g[l]))
        gains[:, b + G_QN:b + G_QN + 4] = f32(mla_q_norm_g[l]).reshape(4, 128).T
        gains[:, b + G_KVN:b + G_KVN + 4] = f32(mla_kv_norm_g[l]).reshape(4, 128).T
        gains[:, b + G_MIXPOST:b + G_MIXPOST + KC] = pack_gain(f32(mix_post_g[l]))
        b += 64
        gains[:, b:b + KC] = pack_gain(f32(ffn2_pre_g[l]))
        gains[:, b + KC:b + 2 * KC] = pack_gain(f32(ffn2_post_g[l]))
    gains[:, G_ROPEC:G_ROPEC + 8] = rope_consts()
    w_in, mla_w_uq, mla_w_ukv = f32(w_in), f32(mla_w_uq), f32(mla_w_ukv)
    pw = [proj_weights(w_in[l], mla_w_uq[l], mla_w_ukv[l]) for l in range(DEPTH)]
    shared = {
        "ffn1_w_gate": f32(ffn1_w_gate), "ffn1_w_up": f32(ffn1_w_up), "ffn1_w_down": f32(ffn1_w_down),
        "ffn2_w_gate": f32(ffn2_w_gate), "ffn2_w_up": f32(ffn2_w_up), "ffn2_w_down": f32(ffn2_w_down),
        "w_in": w_in, "mla_w_uq": mla_w_uq, "mla_w_ukv": mla_w_ukv, "w_o": f32(w_o),
        "w_sw": np.stack([p[0] for p in pw]), "w_uq_sw": np.stack([p[1] for p in pw]),
        "w_vd": np.stack([p[2] for p in pw]), "w_va": np.stack([p[3] for p in pw]),
        "gains": gains,
    }
    if "fused" not in _PROGS:
        _PROGS["fused"] = build_fused()
    nc = _PROGS["fused"]
    in_maps = []
    for c in range(NCORE):
        qidx, kidx, dmask = attn_tables(c)
        m = dict(shared)
        m["x_in"] = np.ascontiguousarray(x[0][toks[c]].T)
        m["pos"] = np.ascontiguousarray(np.broadcast_to(positions[0][toks[c]].astype(np.int32), (128, TOK)))
        m["qidx"], m["kidx"] = qidx, kidx
        m["dmask"] = dmask.astype(mybir.dt.np(BF16))
        in_maps.append(m)
    res = run_bass_kernel_spmd(nc, in_maps, core_ids=list(range(NCORE)))
    out = np.zeros((1, S, D), np.float32)
    for c in range(NCORE):
        out[0, toks[c]] = res.results[c]["x_out"].T
    return out
```

```python
import numpy as np
import concourse.bass as bass
import concourse.mybir as mybir
from concourse.bass_utils import run_bass_kernel_spmd
from contextlib import ExitStack

F32 = mybir.dt.float32
BF16 = mybir.dt.bfloat16
I32 = mybir.dt.int32
AF = mybir.ActivationFunctionType
ALU = mybir.AluOpType

D = 2048
S = 8192
DEPTH = 4
DFF = 5632
NCORE = 8
TOK = 1024
KC = D // 128
FC = DFF // 128
EPS = 1e-6
GW = 64 * 12 + 8
G_ROPEC = 64 * 12


DBG = {}


class Res:
    __slots__ = ("name", "last_w", "readers", "excl")
    ALL = []

    def __init__(self, name, excl=False):
        self.name = name
        self.last_w = None
        self.readers = {}
        self.excl = excl
        Res.ALL.append(self)


class Op:
    __slots__ = ("eng", "fn", "dma", "deps", "idx", "sig", "inc", "waits", "users", "incv")

    def __init__(self, eng, fn, dma):
        self.eng = eng
        self.fn = fn
        self.dma = dma
        self.deps = set()
        self.sig = None
        self.inc = False
        self.waits = []
        self.users = 0
        self.incv = 16


ENGS = ["pe", "act", "dve", "pool", "sp"]


class Prog:
    NDMASEM = 6

    def __init__(self, nc, es):
        self.nc = nc
        self.ops = []
        self.final = []
        self.emitted = 0
        self.csem = {e: es.enter_context(nc.semaphore("c_" + e)) for e in ENGS}
        self.dsem = {e: [es.enter_context(nc.semaphore("d_%s%d" % (e, i))) for i in range(self.NDMASEM)]
                     for e in ("act", "pool", "sp")}
        self.ccsem = [es.enter_context(nc.semaphore("cc%d" % i)) for i in range(2)]
        self.ccount = {e: 0 for e in ENGS}
        self.dcount = {e: 0 for e in self.dsem}
        self.dval = {}
        self.cccount = 0
        self.waited = {e: {} for e in ENGS}
        self.last = {e: None for e in ENGS}
        self.dmas = []

    def add(self, eng, fn, r=(), w=(), dma=False, final=False, cc=False):
        o = Op(eng, fn, dma or cc)
        if cc:
            o.incv = 1
        o.idx = len(self.ops)
        dma = o.dma
        deps = set()
        for res in r:
            d = res.last_w
            if d is not None:
                if d.dma or dma or d.eng != eng or eng != "pe":
                    deps.add(d)
            if res.excl:
                for d in res.readers.values():
                    if d.eng != eng:
                        deps.add(d)
        for res in w:
            d = res.last_w
            if d is not None and (d.dma or dma or d.eng != eng or eng != "pe"):
                deps.add(d)
            for d in res.readers.values():
                if d is not o and (d.dma or dma or d.eng != eng or eng != "pe"):
                    deps.add(d)
        for res in r:
            key = ("dma", o.idx) if dma else eng
            res.readers[key] = o
        for res in w:
            res.last_w = o
            res.readers = {}
        o.deps = deps
        for d in deps:
            d.users += 1
        self.ops.append(o)
        if dma:
            self.dmas.append(o)
        else:
            self.last[eng] = o
        if final:
            self.final.append(o)
        return o

    def barrier(self):
        tails = [o for o in self.last.values() if o is not None] + list(self.dmas)
        for e in ENGS:
            o = Op(e, None, False)
            o.idx = len(self.ops)
            o.deps = set(t for t in tails if t.eng != e or t.dma)
            for d in o.deps:
                d.users += 1
            self.ops.append(o)
        self.dmas = []
        for res in Res.ALL:
            res.last_w = None
            res.readers = {}

    def finish(self):
        fin = Op("sp", None, False)
        fin.deps = set(self.final)
        for d in fin.deps:
            d.users += 1
        fin.idx = len(self.ops)
        self.ops.append(fin)
        self.final = []

    def emit(self):
        nc = self.nc
        new = self.ops[self.emitted:]
        self.emitted = len(self.ops)
        for o in new:
            waits = {}
            if o.dma and o.incv == 1:
                s = self.ccsem[self.cccount % 2]
                self.cccount += 1
                prev = self.dval.get(id(s), 0)
                if prev > 0:
                    waits[id(s)] = (s, prev)
                self.dval[id(s)] = prev + 1
                o.sig = (s, prev + 1)
            elif o.dma:
                i = self.dcount[o.eng]
                self.dcount[o.eng] += 1
                s = self.dsem[o.eng][i % self.NDMASEM]
                prev = self.dval.get(id(s), 0)
                if prev > 0:
                    waits[id(s)] = (s, prev)
                self.dval[id(s)] = prev + 16
                o.sig = (s, prev + 16)
            elif o.users > 0 and o.fn is not None:
                self.ccount[o.eng] += 1
                o.sig = (self.csem[o.eng], self.ccount[o.eng])
                o.inc = True
            for d in o.deps:
                if d.sig is None:
                    continue
                s, v = d.sig
                if id(s) not in waits or waits[id(s)][1] < v:
                    waits[id(s)] = (s, v)
            o.waits = list(waits.values())
        with nc.Block() as block:
            for name, deco in (("pe", block.tensor), ("act", block.scalar), ("dve", block.vector),
                               ("pool", block.gpsimd), ("sp", block.sync)):
                ops = [o for o in new if o.eng == name]
                if not ops:
                    continue

                def body(e, ops=ops, name=name):
                    waited = self.waited[name]
                    for o in ops:
                        for (s, v) in o.waits:
                            if waited.get(id(s), 0) < v:
                                e.wait_ge(s, v)
                                waited[id(s)] = v
                        if o.fn is None:
                            continue
                        ins = o.fn(e)
                        if o.dma:
                            ins.then_inc(o.sig[0], o.incv)
                        elif o.inc:
                            ins.then_inc(o.sig[0], 1)

                deco(body)


class Ctx:
    pass


def make_ctx(nc, es):
    c = Ctx()
    c.nc = nc

    c.es = es
    c.uid = [0]

    def sb(name, shape, dt):
        c.uid[0] += 1
        return c.es.enter_context(nc.sbuf_tensor("sb%d_%s" % (c.uid[0], name), shape, dt))

    c.sb = sb
    c.gb = 0
    c.big = sb("big", [128, KC, TOK], F32)
    c.r_big = [Res("big%d" % i) for i in range(KC)]
    c.h = sb("h", [128, KC, TOK], BF16)
    c.r_h = [Res("h%d" % i) for i in range(KC)]
    c.ones = sb("ones", [128, 128], BF16)
    c.r_ones = Res("ones")
    c.rstd = sb("rstd", [128, TOK], F32)
    c.r_rstd = Res("rstd")
    c.sq = [sb("sq%d" % i, [128, TOK], BF16) for i in range(2)]
    c.r_sq = [Res("sq%d" % i) for i in range(2)]
    c.ps = [es.enter_context(nc.psum_tensor("ps%d" % i, [128, 512], F32)) for i in range(8)]
    c.r_ps = [Res("ps%d" % i, excl=True) for i in range(8)]
    c.gains = sb("gains", [128, GW], F32)
    c.tabs = {k: sb("tab_" + k, [64, TOK], F32) for k in ("cosA", "sinA", "cosP", "sinP")}
    c.r_tabs = {k: Res("tab_" + k) for k in c.tabs}
    c.r_gains = Res("gains")
    return c


def load_consts(P, c):
    P.add("dve", lambda e: e.memset(c.ones[:], 1.0), w=[c.r_ones])


def rms_stats(P, c, src_chunks, r_src, nchunks, dim, out_rstd, r_out, ps_banks=(0, 1)):
    for i in range(nchunks):
        s = i % 2
        P.add("act", lambda e, i=i, s=s: e.activation(out=c.sq[s][:], in_=src_chunks(i), func=AF.Square),
              r=[r_src[i]], w=[c.r_sq[s]])
        for b in range(2):
            P.add("pe", lambda e, i=i, s=s, b=b: e.matmul(
                out=c.ps[ps_banks[b]][:], lhsT=c.ones[:], rhs=c.sq[s][:, b * 512:(b + 1) * 512],
                start=(i == 0), stop=(i == nchunks - 1)),
                r=[c.r_sq[s], c.r_ones], w=[c.r_ps[ps_banks[b]]])
    for b in range(2):
        P.add("dve", lambda e, b=b: e.tensor_scalar(
            out=out_rstd[:, b * 512:(b + 1) * 512], in0=c.ps[ps_banks[b]][:],
            scalar1=1.0 / dim, scalar2=EPS, op0=ALU.mult, op1=ALU.add),
            r=[c.r_ps[ps_banks[b]]], w=[r_out])
    P.add("act", lambda e: e.activation(out=out_rstd[:], in_=out_rstd[:], func=AF.Sqrt),
          r=[r_out], w=[r_out])
    P.add("dve", lambda e: e.reciprocal(out=out_rstd[:], in_=out_rstd[:]),
          r=[r_out], w=[r_out])


NFS = 4
FPS = FC // NFS


def alloc_ffn(c):
    sb = c.sb
    c.act = sb("ffn_act", [128, FPS, TOK], BF16)
    c.r_act = [Res("act%d" % i) for i in range(FPS)]
    c.wgu = [sb("wgu%d" % i, [128, 2, KC, 128], BF16) for i in range(3)]
    c.r_wgu = [Res("wgu%d" % i) for i in range(3)]
    c.wd = [sb("wd%d" % i, [128, FPS, 128], BF16) for i in range(3)]
    c.r_wd = [Res("wd%d" % i) for i in range(3)]
    c.sil = [sb("sil%d" % i, [128, 512], F32) for i in range(4)]
    c.r_sil = [Res("sil%d" % i) for i in range(4)]
    c.xb = [sb("xb%d" % i, [128, TOK], F32) for i in range(2)]
    c.r_xb = [Res("xb%d" % i) for i in range(2)]
    c.yb = [sb("yb%d" % i, [128, TOK], F32) for i in range(2)]
    c.r_yb = [Res("yb%d" % i) for i in range(2)]
    c.ghalf = sb("ghalf", [128, KC], F32)
    c.r_ghalf = Res("ghalf")


def ffn_phase(P, c, x_in, x_out, wg, wu, wd, gcol_pre, gcol_post, final=False):
    xin_v = x_in.rearrange("(kc kp) t -> kp kc t", kp=128)
    xout_v = x_out.rearrange("(kc kp) t -> kp kc t", kp=128)
    for q in range(4):
        P.add("sp", lambda e, q=q: e.dma_start(out=c.big[:, 4 * q:4 * q + 4, :], in_=xin_v[:, 4 * q:4 * q + 4, :]),
              w=c.r_big[4 * q:4 * q + 4], dma=True)
    rms_stats(P, c, lambda i: c.big[:, i, :], c.r_big, KC, D, c.rstd, c.r_rstd)
    for i in range(KC):
        P.add("dve", lambda e, i=i: e.scalar_tensor_tensor(
            out=c.h[:, i, :], in0=c.big[:, i, :], scalar=c.gains[:, gcol_pre + i:gcol_pre + i + 1],
            in1=c.rstd[:], op0=ALU.mult, op1=ALU.mult),
            r=[c.r_big[i], c.r_rstd, c.r_gains], w=[c.r_h[i]])
    P.add("dve", lambda e: e.tensor_scalar(out=c.ghalf[:], in0=c.gains[:, gcol_post:gcol_post + KC],
                                           scalar1=0.5, scalar2=None, op0=ALU.mult),
          r=[c.r_gains], w=[c.r_ghalf])
    wslot = 0
    dslot = 0
    bank = 0
    for fs in range(NFS):
        for jc in range(FPS):
            f0 = (fs * FPS + jc) * 128
            ws = wslot % 3
            wslot += 1
            for t, wsrc in enumerate((wg, wu)):
                P.add("pool", lambda e, ws=ws, t=t, wsrc=wsrc, f0=f0: e.dma_start(
                    out=c.wgu[ws][:, t, :, :],
                    in_=wsrc[:, f0:f0 + 128].rearrange("(kc kp) f -> kp kc f", kp=128)),
                    w=[c.r_wgu[ws]], dma=True)
            pb = (jc % 2) * 4
            for t in range(2):
                for k in range(KC):
                    for b in range(2):
                        P.add("pe", lambda e, ws=ws, t=t, k=k, b=b, pb=pb: e.matmul(
                            out=c.ps[pb + 2 * t + b][:], lhsT=c.wgu[ws][:, t, k, :],
                            rhs=c.h[:, k, b * 512:(b + 1) * 512], start=(k == 0), stop=(k == KC - 1)),
                            r=[c.r_wgu[ws], c.r_h[k]], w=[c.r_ps[pb + 2 * t + b]])
            for b in range(2):
                sl = (2 * jc + b) % 4
                P.add("act", lambda e, sl=sl, pb=pb, b=b: e.activation(
                    out=c.sil[sl][:], in_=c.ps[pb + b][:], func=AF.Silu),
                    r=[c.r_ps[pb + b]], w=[c.r_sil[sl]])
                P.add("dve", lambda e, sl=sl, pb=pb, b=b, jc=jc: e.tensor_tensor(
                    out=c.act[:, jc, b * 512:(b + 1) * 512], in0=c.sil[sl][:], in1=c.ps[pb + 2 + b][:],
                    op=ALU.mult),
                    r=[c.r_sil[sl], c.r_ps[pb + 2 + b]], w=[c.r_act[jc]])
        for i in range(KC if DBG.get('p2', 1) else 0):
            ds_ = dslot % 3
            dslot += 1
            r0 = fs * FPS * 128
            P.add("pool", lambda e, ds_=ds_, i=i, r0=r0: e.dma_start(
                out=c.wd[ds_][:],
                in_=wd[r0:r0 + FPS * 128, i * 128:(i + 1) * 128].rearrange("(jc p) d -> p jc d", p=128)),
                w=[c.r_wd[ds_]], dma=True)
            for b in range(2):
                bk = bank % 8
                bank += 1
                for jc in range(FPS):
                    P.add("pe", lambda e, ds_=ds_, jc=jc, b=b, bk=bk: e.matmul(
                        out=c.ps[bk][:], lhsT=c.wd[ds_][:, jc, :], rhs=c.act[:, jc, b * 512:(b + 1) * 512],
                        start=(jc == 0), stop=(jc == FPS - 1)),
                        r=[c.r_wd[ds_], c.r_act[jc]], w=[c.r_ps[bk]])
                if fs == 0:
                    P.add("act", lambda e, i=i, b=b, bk=bk: e.copy(
                        out=c.big[:, i, b * 512:(b + 1) * 512], in_=c.ps[bk][:]),
                        r=[c.r_ps[bk]], w=[c.r_big[i]])
                else:
                    P.add("dve", lambda e, i=i, b=b, bk=bk: e.tensor_tensor(
                        out=c.big[:, i, b * 512:(b + 1) * 512], in0=c.big[:, i, b * 512:(b + 1) * 512],
                        in1=c.ps[bk][:], op=ALU.add),
                        r=[c.r_ps[bk], c.r_big[i]], w=[c.r_big[i]])
    if DBG.get('post', 1):
        rms_stats(P, c, lambda i: c.big[:, i, :], c.r_big, KC, D, c.rstd, c.r_rstd)
    else:
        for i in range(KC):
            P.add('sp', lambda e, i=i: e.dma_start(out=xout_v[:, i, :], in_=c.big[:, i, :]), r=[c.r_big[i]], dma=True, final=final)
    for i in range(KC if DBG.get('post', 1) else 0):
        xs = i % len(c.xb)
        ys = i % 2
        P.add("sp", lambda e, i=i, xs=xs: e.dma_start(out=c.xb[xs][:], in_=xin_v[:, i, :]),
              w=[c.r_xb[xs]], dma=True)
        P.add("dve", lambda e, i=i, ys=ys: e.scalar_tensor_tensor(
            out=c.yb[ys][:], in0=c.big[:, i, :], scalar=c.ghalf[:, i:i + 1], in1=c.rstd[:],
            op0=ALU.mult, op1=ALU.mult),
            r=[c.r_big[i], c.r_ghalf, c.r_rstd], w=[c.r_yb[ys]])
        P.add("dve", lambda e, xs=xs, ys=ys: e.tensor_tensor(
            out=c.yb[ys][:], in0=c.yb[ys][:], in1=c.xb[xs][:], op=ALU.add),
            r=[c.r_yb[ys], c.r_xb[xs]], w=[c.r_yb[ys]])
        P.add("sp", lambda e, i=i, ys=ys: e.dma_start(out=xout_v[:, i, :], in_=c.yb[ys][:]),
              r=[c.r_yb[ys]], dma=True, final=final)


def build_ffn_launch():
    nc = bass.Bass("TRN2", target_bir_lowering=False)
    x_in = nc.dram_tensor("x_in", [D, TOK], F32, kind="ExternalInput").ap()
    wg = nc.dram_tensor("wg", [D, DFF], F32, kind="ExternalInput").ap()
    wu = nc.dram_tensor("wu", [D, DFF], F32, kind="ExternalInput").ap()
    wd = nc.dram_tensor("wd", [DFF, D], F32, kind="ExternalInput").ap()
    gains = nc.dram_tensor("gains", [128, 64], F32, kind="ExternalInput").ap()
    x_out = nc.dram_tensor("x_out", [D, TOK], F32, kind="ExternalOutput").ap()
    with ExitStack() as es:
        c = make_ctx(nc, es)
        alloc_ffn(c)
        P = Prog(nc, es)
        load_consts(P, c)
        P.add("sp", lambda e: e.dma_start(out=c.gains[:, 0:64], in_=gains), w=[c.r_gains], dma=True)
        ffn_phase(P, c, x_in, x_out, wg, wu, wd, 0, KC, final=True)
        P.finish()
        P.emit()
    return nc


def tok_index(core):
    a = np.arange(512 * core, 512 * core + 512)
    b = np.arange(512 * (15 - core), 512 * (15 - core) + 512)
    return np.concatenate([a, b])


def pack_gain(g):
    return np.ascontiguousarray(g.reshape(KC, 128).T)


PI = float(np.pi)
NH = 8
W_IN_COLS = 4160
G_MIXPRE, G_QN, G_KVN, G_ROPE, G_MIXPOST = 0, 16, 20, 32, 40


def alloc_proj(c):
    sb = c.sb
    c.wp = [sb("wp%d" % i, [128, KC, 128], BF16) for i in range(3)]
    c.r_wp = [Res("wp%d" % i) for i in range(3)]
    c.wv = [sb("wv%d" % i, [128, KC, 512], BF16) for i in range(1)]
    c.r_wv = [Res("wv%d" % i) for i in range(1)]
    c.st = [sb("st%d" % i, [128, TOK], BF16) for i in range(3)]
    c.r_st = [Res("st%d" % i) for i in range(3)]
    c.t1 = [sb("t1_%d" % i, [64, 512], F32) for i in range(2)]
    c.r_t1 = [Res("t1_%d" % i) for i in range(2)]
    c.t2 = [sb("t2_%d" % i, [64, 512], F32) for i in range(2)]
    c.r_t2 = [Res("t2_%d" % i) for i in range(2)]
    c.lat = sb("lat", [128, 8, TOK], BF16)
    c.r_lat = [Res("lat%d" % i) for i in range(8)]
    c.vst = [sb("vst%d" % i, [128, 1024], BF16) for i in range(2)]
    c.r_vst = [Res("vst%d" % i) for i in range(2)]


def alloc_rope(c):
    sb = c.sb
    c.posi = sb("posi", [128, TOK], I32)
    c.ta = sb("ta", [64, TOK], F32)
    c.tb = sb("tb", [64, TOK], F32)
    c.tk = sb("tk", [64, TOK], I32)
    c.r_ta, c.r_tb, c.r_tk = Res("ta"), Res("tb"), Res("tk")


def rope_tables(P, c, pos_rep):
    r_pos = Res("posi")
    P.add("sp", lambda e: e.dma_start(out=c.posi[:], in_=pos_rep), w=[r_pos], dma=True)
    for nm, nrows, col in (("A", 64, G_ROPEC), ("P", 32, G_ROPEC + 2)):
        P.add("dve", lambda e, nrows=nrows: e.tensor_copy(out=c.ta[0:nrows, :], in_=c.posi[0:nrows, :]),
              r=[r_pos], w=[c.r_ta])
        P.add("dve", lambda e, nrows=nrows, col=col: e.tensor_scalar(
            out=c.ta[0:nrows, :], in0=c.ta[0:nrows, :], scalar1=c.gains[0:nrows, col:col + 1], scalar2=None,
            op0=ALU.mult), r=[c.r_ta, c.r_gains], w=[c.r_ta])
        for kind, shift in (("sin", 0.0), ("cos", PI / 2)):
            tab = c.tabs[kind + nm]
            rt = c.r_tabs[kind + nm]
            P.add("dve", lambda e, nrows=nrows, shift=shift: e.tensor_scalar(
                out=c.tb[0:nrows, :], in0=c.ta[0:nrows, :], scalar1=shift, scalar2=None, op0=ALU.add),
                r=[c.r_ta], w=[c.r_tb])
            P.add("dve", lambda e, nrows=nrows, tab=tab: e.tensor_scalar(
                out=tab[0:nrows, :], in0=c.tb[0:nrows, :], scalar1=1.0 / (2 * PI), scalar2=None, op0=ALU.mult),
                r=[c.r_tb], w=[rt])
            P.add("dve", lambda e, nrows=nrows, tab=tab: e.tensor_copy(out=c.tk[0:nrows, :], in_=tab[0:nrows, :]),
                  r=[rt], w=[c.r_tk])
            P.add("dve", lambda e, nrows=nrows, tab=tab: e.tensor_copy(out=tab[0:nrows, :], in_=c.tk[0:nrows, :]),
                  r=[c.r_tk], w=[rt])
            P.add("dve", lambda e, nrows=nrows, tab=tab: e.scalar_tensor_tensor(
                out=c.tb[0:nrows, :], in0=tab[0:nrows, :], scalar=-2 * PI, in1=c.tb[0:nrows, :],
                op0=ALU.mult, op1=ALU.add), r=[rt, c.r_tb], w=[c.r_tb])
            P.add("dve", lambda e, nrows=nrows, tab=tab: e.tensor_scalar(
                out=tab[0:nrows, :], in0=c.tb[0:nrows, :], scalar1=PI, scalar2=-2 * PI, op0=ALU.is_gt, op1=ALU.mult),
                r=[c.r_tb], w=[rt])
            P.add("dve", lambda e, nrows=nrows, tab=tab: e.tensor_tensor(
                out=c.tb[0:nrows, :], in0=c.tb[0:nrows, :], in1=tab[0:nrows, :], op=ALU.add),
                r=[rt, c.r_tb], w=[c.r_tb])
            P.add("dve", lambda e, nrows=nrows, tab=tab: e.tensor_scalar(
                out=tab[0:nrows, :], in0=c.tb[0:nrows, :], scalar1=-PI, scalar2=2 * PI, op0=ALU.is_lt, op1=ALU.mult),
                r=[c.r_tb], w=[rt])
            P.add("dve", lambda e, nrows=nrows, tab=tab: e.tensor_tensor(
                out=c.tb[0:nrows, :], in0=c.tb[0:nrows, :], in1=tab[0:nrows, :], op=ALU.add),
                r=[rt, c.r_tb], w=[c.r_tb])
            P.add("act", lambda e, nrows=nrows, tab=tab: e.activation(out=tab[0:nrows, :], in_=c.tb[0:nrows, :], func=AF.Sin),
                  r=[c.r_tb], w=[rt])
            if kind == "sin":
                P.add("dve", lambda e, nrows=nrows, tab=tab, col=col: e.tensor_scalar(
                    out=tab[0:nrows, :], in0=tab[0:nrows, :], scalar1=c.gains[0:nrows, col + 1:col + 2], scalar2=None,
                    op0=ALU.mult), r=[rt, c.r_gains], w=[rt])


class Rot:
    def __init__(self):
        self.w = 0
        self.bank = 0
        self.st = 0
        self.t = 0
        self.v = 0
        self.vs = 0


def lin_fm(P, c, rot, wsrc, c0, M, nk, rhs, r_rhs, evac):
    ws = rot.w % 3
    rot.w += 1
    P.add("pool", lambda e: e.dma_start(
        out=c.wp[ws][:, 0:nk, 0:M], in_=wsrc[:, c0:c0 + M].rearrange("(kc kp) f -> kp kc f", kp=128)),
        w=[c.r_wp[ws]], dma=True)
    banks = []
    for b in range(2):
        bk = rot.bank % 8
        rot.bank += 1
        banks.append(bk)
    for k in range(nk):
        for b in range(2):
            P.add("pe", lambda e, k=k, b=b: e.matmul(
                out=c.ps[banks[b]][0:M, :], lhsT=c.wp[ws][:, k, 0:M], rhs=rhs[:, k, b * 512:(b + 1) * 512],
                start=(k == 0), stop=(k == nk - 1)),
                r=[c.r_wp[ws], r_rhs[k]], w=[c.r_ps[banks[b]]])
    for b in range(2):
        evac(b, banks[b])
    return banks


def proj_phase(P, c, x_in, w_in, w_sw, w_uq, w_uq_sw, w_ukv, w_vd, w_va, outs):
    rot = Rot()
    xin_v = x_in.rearrange("(kc kp) t -> kp kc t", kp=128)
    for q in range(4):
        P.add("sp", lambda e, q=q: e.dma_start(out=c.big[:, 4 * q:4 * q + 4, :], in_=xin_v[:, 4 * q:4 * q + 4, :]),
              w=c.r_big[4 * q:4 * q + 4], dma=True)
    rms_stats(P, c, lambda i: c.big[:, i, :], c.r_big, KC, D, c.rstd, c.r_rstd)
    for i in range(KC):
        P.add("dve", lambda e, i=i: e.scalar_tensor_tensor(
            out=c.h[:, i, :], in0=c.big[:, i, :], scalar=c.gains[:, c.gb + G_MIXPRE + i:c.gb + G_MIXPRE + i + 1],
            in1=c.rstd[:], op0=ALU.mult, op1=ALU.mult),
            r=[c.r_big[i], c.r_rstd, c.r_gains], w=[c.r_h[i]])

    def store(dst, rows, st):
        P.add("sp", lambda e: e.dma_start(out=dst, in_=c.st[st][0:rows, :]), r=[c.r_st[st]], dma=True)

    def evac_plain(dst, rows):
        st = rot.st % 3
        rot.st += 1

        def ev(b, bk):
            P.add("act", lambda e: e.copy(out=c.st[st][0:rows, b * 512:(b + 1) * 512], in_=c.ps[bk][0:rows, :]),
                  r=[c.r_ps[bk]], w=[c.r_st[st]])
            if b == 1:
                store(dst, rows, st)
        return ev

    def rope_chunk(wmain, cm, M, wswp, cs, R, nk, rhs, r_rhs, tabn, dst):
        st = rot.st % 3
        rot.st += 1
        held = {}

        def ev_main(b, bk):
            held[b] = bk
        lin_fm(P, c, rot, wmain, cm, M, nk, rhs, r_rhs, ev_main)

        def ev_sw(b, bk2):
            bk = held[b]
            t = rot.t % 2
            rot.t += 1
            cs_, sn_ = c.tabs["cos" + tabn], c.tabs["sin" + tabn]
            P.add("dve", lambda e: e.tensor_tensor(
                out=c.t1[t][0:R, :], in0=c.ps[bk][0:R, :], in1=cs_[0:R, b * 512:(b + 1) * 512], op=ALU.mult),
                r=[c.r_ps[bk], c.r_tabs["cos" + tabn]], w=[c.r_t1[t]])
            P.add("dve", lambda e: e.tensor_tensor(
                out=c.t2[t][0:R, :], in0=c.ps[bk2][0:R, :], in1=sn_[0:R, b * 512:(b + 1) * 512], op=ALU.mult),
                r=[c.r_ps[bk2], c.r_tabs["sin" + tabn]], w=[c.r_t2[t]])
            if M > R:
                P.add("act", lambda e: e.copy(out=c.st[st][0:M, b * 512:(b + 1) * 512], in_=c.ps[bk][0:M, :]),
                      r=[c.r_ps[bk]], w=[c.r_st[st]])
            P.add("dve", lambda e: e.tensor_tensor(
                out=c.st[st][0:R, b * 512:(b + 1) * 512], in0=c.t1[t][0:R, :], in1=c.t2[t][0:R, :], op=ALU.add),
                r=[c.r_t1[t], c.r_t2[t]], w=[c.r_st[st]])
            if b == 1:
                store(dst, M, st)
        lin_fm(P, c, rot, wswp, cs, R, nk, rhs, r_rhs, ev_sw)

    for j in range(8):
        def ev(b, bk, j=j):
            P.add("act", lambda e: e.copy(out=c.big[:, j, b * 512:(b + 1) * 512], in_=c.ps[bk][:]),
                  r=[c.r_ps[bk]] + c.r_h, w=[c.r_big[j]])
        lin_fm(P, c, rot, w_in, j * 128, 128, KC, c.h, c.r_h, ev)
    for grp, gcol in ((0, c.gb + G_QN), (1, c.gb + G_KVN)):
        rms_stats(P, c, lambda i, grp=grp: c.big[:, 4 * grp + i, :], c.r_big[4 * grp:4 * grp + 4], 4, 512,
                  c.rstd, c.r_rstd, ps_banks=(2 * grp, 2 * grp + 1))
        for i in range(4):
            P.add("dve", lambda e, i=i, grp=grp, gcol=gcol: e.scalar_tensor_tensor(
                out=c.lat[:, 4 * grp + i, :], in0=c.big[:, 4 * grp + i, :], scalar=c.gains[:, gcol + i:gcol + i + 1],
                in1=c.rstd[:], op0=ALU.mult, op1=ALU.mult),
                r=[c.r_big[4 * grp + i], c.r_rstd, c.r_gains], w=[c.r_lat[4 * grp + i]])
    latq, r_latq = c.lat[:, 0:4, :], c.r_lat[0:4]
    latkv, r_latkv = c.lat[:, 4:8, :], c.r_lat[4:8]
    rope_chunk(w_in, 1024, 64, w_sw, 0, 64, KC, c.h, c.r_h, "A", outs["kr"])
    for hd in range(NH):
        rope_chunk(w_in, 1088 + hd * 128, 128, w_sw, 64 + hd * 32, 32, KC, c.h, c.r_h, "P", outs["qd"][hd])
        rope_chunk(w_in, 2112 + hd * 128, 128, w_sw, 64 + 256 + hd * 32, 32, KC, c.h, c.r_h, "P", outs["kd"][hd])
    for hd in range(NH):
        lin_fm(P, c, rot, w_uq, hd * 192, 128, 4, latq, r_latq, evac_plain(outs["qn"][hd], 128))
        rope_chunk(w_uq, hd * 192 + 128, 64, w_uq_sw, hd * 64, 64, 4, latq, r_latq, "A", outs["qr"][hd])
        lin_fm(P, c, rot, w_ukv, hd * 256, 128, 4, latkv, r_latkv, evac_plain(outs["kn"][hd], 128))
    for (wsrc, nk, lhs, r_lhs, dst) in ((w_vd, KC, c.h, c.r_h, outs["vd"]), (w_va, 4, latkv, r_latkv, outs["va"])):
        for half in range(2):
            v = 0
            P.add("pool", lambda e, v=v, wsrc=wsrc, nk=nk, half=half: e.dma_start(
                out=c.wv[v][:, 0:nk, :],
                in_=wsrc[:, half * 512:(half + 1) * 512].rearrange("(kc kp) f -> kp kc f", kp=128)),
                w=[c.r_wv[v]], dma=True)
            for tt in range(TOK // 128):
                bk = rot.bank % 8
                rot.bank += 1
                for k in range(nk):
                    P.add("pe", lambda e, k=k, tt=tt, bk=bk, v=v, lhs=lhs, nk=nk: e.matmul(
                        out=c.ps[bk][:], lhsT=lhs[:, k, tt * 128:(tt + 1) * 128], rhs=c.wv[v][:, k, :],
                        start=(k == 0), stop=(k == nk - 1)),
                        r=[c.r_wv[v], r_lhs[k]], w=[c.r_ps[bk]])
                vs = rot.vs % 2
                rot.vs += 1
                P.add("act", lambda e, bk=bk, vs=vs: e.copy(out=c.vst[vs][:, 0:512], in_=c.ps[bk][:]),
                      r=[c.r_ps[bk]], w=[c.r_vst[vs]])
                P.add("sp", lambda e, vs=vs, tt=tt, half=half, dst=dst: e.dma_start(
                    out=dst[tt * 128:(tt + 1) * 128, half * 512:(half + 1) * 512], in_=c.vst[vs][:, 0:512]),
                    r=[c.r_vst[vs]], dma=True)


def rope_consts():
    g = np.zeros((128, 8), np.float32)
    inv_a = (500000.0 ** (-np.arange(0, 64, 2, dtype=np.float32) / 64)).astype(np.float32)
    inv_p = (500000.0 ** (-np.arange(0, 32, 2, dtype=np.float32) / 32)).astype(np.float32)
    g[0:64, 0] = np.concatenate([inv_a, inv_a])
    g[0:64, 1] = np.concatenate([-np.ones(32), np.ones(32)])
    g[0:32, 2] = np.concatenate([inv_p, inv_p])
    g[0:32, 3] = np.concatenate([-np.ones(16), np.ones(16)])
    return g


def proj_weights(w_in, w_uq, w_ukv):
    sw = [w_in[:, 1024 + 32:1024 + 64], w_in[:, 1024:1024 + 32]]
    for base in (1088, 2112):
        for hd in range(NH):
            c0 = base + hd * 128
            sw += [w_in[:, c0 + 16:c0 + 32], w_in[:, c0:c0 + 16]]
    w_sw = np.ascontiguousarray(np.concatenate(sw, axis=1))
    uqs = []
    for hd in range(NH):
        c0 = hd * 192 + 128
        uqs += [w_uq[:, c0 + 32:c0 + 64], w_uq[:, c0:c0 + 32]]
    w_uq_sw = np.ascontiguousarray(np.concatenate(uqs, axis=1))
    w_vd = np.ascontiguousarray(w_in[:, 3136:4160])
    w_va = np.ascontiguousarray(np.concatenate([w_ukv[:, hd * 256 + 128:hd * 256 + 256] for hd in range(NH)], axis=1))
    return w_sw, w_uq_sw, w_vd, w_va


NPT = 4
KT_ROWS = 2112


def alloc_attn(c):
    sb = c.sb
    c.kn = sb("kn", [128, NCORE, TOK], BF16)
    c.kr = sb("kr", [64, NCORE, TOK], BF16)
    c.va = sb("va", [128, 64, 128], BF16)
    c.r_kn, c.r_kr, c.r_va = Res("kn"), Res("kr"), Res("va")
    c.q1 = sb("q1", [128, TOK], BF16)
    c.q2 = sb("q2", [64, TOK], BF16)
    c.r_q1, c.r_q2 = Res("q1"), Res("q2")
    c.pt = [sb("pt%d" % i, [128, 512], BF16) for i in range(NPT)]
    c.r_pt = [Res("pt%d" % i) for i in range(NPT)]
    c.pm = [sb("pm%d" % i, [128, 512], BF16) for i in range(NPT)]
    c.r_pm = [Res("pm%d" % i) for i in range(NPT)]
    c.dm = [sb("dm%d" % i, [128, 512], BF16) for i in range(3)]
    c.r_dm = [Res("dm%d" % i) for i in range(3)]
    c.qidx = sb("qidx", [128, TOK], F32)
    c.kidx = sb("kidx", [128, 64], F32)
    c.r_idx = Res("idx")
    c.rden = sb("rden", [128, 512], F32)
    c.r_rden = Res("rden")


def alloc_attn_out(c):
    sb = c.sb
    c.wp = [sb("wp%d" % i, [128, KC, 128], BF16) for i in range(2)]
    c.r_wp = [Res("wp%d" % i) for i in range(2)]
    c.xb = [sb("xb%d" % i, [128, TOK], F32) for i in range(2)]
    c.r_xb = [Res("xb%d" % i) for i in range(2)]
    c.yb = [sb("yb%d" % i, [128, TOK], F32) for i in range(2)]
    c.r_yb = [Res("yb%d" % i) for i in range(2)]


def key_tiles(b):
    return [(r, lt) for r in range(NCORE) for lt in range(4 if b == 0 else 8)]


def attn_phase(P, c, G):
    KTv = G["KT_g"].rearrange("(r row) t -> row r t", r=NCORE)
    Vg = G["V_g"]
    P.add("sp", lambda e: e.dma_start(out=c.qidx[:], in_=G["qidx"]), w=[c.r_idx], dma=True)
    P.add("sp", lambda e: e.dma_start(out=c.kidx[:], in_=G["kidx"]), w=[c.r_idx], dma=True)
    P.add("sp", lambda e: e.dma_start(out=c.kr[:], in_=KTv[2048:2112]), w=[c.r_kr], dma=True)
    st = {"sb": 0, "p": 0, "acc": 0, "dm": 0}
    SB = (0, 1, 6, 7)

    def core(b, score_mm, scale, mask_op, out_chunk):
        tl = key_tiles(b)
        ob, db = (2, 3) if st["acc"] % 2 == 0 else (4, 5)
        st["acc"] += 1
        LA = 2
        n = len(tl)
        slots = {}
        for step in range(n + LA):
            if step < n:
                r, lt = tl[step]
                sbk = SB[st["sb"] % 4]
                st["sb"] += 1
                ps_ = st["p"] % NPT
                st["p"] += 1
                slots[step] = ps_
                score_mm(r, lt, sbk)
                P.add("act", lambda e, sbk=sbk, ps_=ps_: e.activation(
                    out=c.pt[ps_][:], in_=c.ps[sbk][:], func=AF.Exp, scale=scale),
                    r=[c.r_ps[sbk]], w=[c.r_pt[ps_]])
                mask_op(r * 8 + lt, ps_)
            ti = step - LA
            if ti >= 0:
                r, lt = tl[ti]
                ps_ = slots[ti]
                first, last = (ti == 0), (ti == n - 1)
                P.add("pe", lambda e, ps_=ps_, k=r * 8 + lt, first=first, last=last: e.matmul(
                    out=c.ps[ob][:], lhsT=c.va[:, k, :], rhs=c.pm[ps_][:], start=first, stop=last),
                    r=[c.r_va, c.r_pm[ps_]], w=[c.r_ps[ob]])
                P.add("pe", lambda e, ps_=ps_, first=first, last=last: e.matmul(
                    out=c.ps[db][:], lhsT=c.ones[:], rhs=c.pm[ps_][:], start=first, stop=last),
                    r=[c.r_ones, c.r_pm[ps_]], w=[c.r_ps[db]])
        P.add("dve", lambda e: e.reciprocal(out=c.rden[:], in_=c.ps[db][:]), r=[c.r_ps[db]], w=[c.r_rden])
        P.add("dve", lambda e: e.tensor_tensor(
            out=c.h[:, out_chunk, b * 512:(b + 1) * 512], in0=c.ps[ob][:], in1=c.rden[:], op=ALU.mult),
            r=[c.r_ps[ob], c.r_rden], w=[c.r_h[out_chunk]])

    for mixer in range(2):
        for hd in range(NH):
            krow = (0 if mixer == 0 else 1024) + hd * 128
            vcol = (0 if mixer == 0 else 1024) + hd * 128
            P.add("sp", lambda e, krow=krow: e.dma_start(out=c.kn[:], in_=KTv[krow:krow + 128]), w=[c.r_kn], dma=True)
            P.add("sp", lambda e, vcol=vcol: e.dma_start(
                out=c.va[:], in_=Vg[:, vcol:vcol + 128].rearrange("(k p) v -> p k v", p=128)), w=[c.r_va], dma=True)
            qsrc = G["qn"][hd] if mixer == 0 else G["qd"][hd]
            P.add("sp", lambda e, qsrc=qsrc: e.dma_start(out=c.q1[:], in_=qsrc), w=[c.r_q1], dma=True)
            if mixer == 0:
                P.add("sp", lambda e, hd=hd: e.dma_start(out=c.q2[:], in_=G["qr"][hd]), w=[c.r_q2], dma=True)
            for b in range(2):
                if mixer == 0:
                    def score_mm(r, lt, sbk, b=b):
                        P.add("pe", lambda e: e.matmul(
                            out=c.ps[sbk][:], lhsT=c.kn[:, r, lt * 128:(lt + 1) * 128],
                            rhs=c.q1[:, b * 512:(b + 1) * 512], start=True, stop=False),
                            r=[c.r_kn, c.r_q1], w=[c.r_ps[sbk]])
                        P.add("pe", lambda e: e.matmul(
                            out=c.ps[sbk][:], lhsT=c.kr[0:64, r, lt * 128:(lt + 1) * 128],
                            rhs=c.q2[0:64, b * 512:(b + 1) * 512], start=False, stop=True),
                            r=[c.r_kr, c.r_q2], w=[c.r_ps[sbk]])

                    def mask_op(k, ps_, b=b):
                        P.add("dve", lambda e: e.scalar_tensor_tensor(
                            out=c.pm[ps_][:], in0=c.qidx[:, b * 512:(b + 1) * 512], scalar=c.kidx[:, k:k + 1],
                            in1=c.pt[ps_][:], op0=ALU.is_ge, op1=ALU.mult),
                            r=[c.r_idx, c.r_pt[ps_]], w=[c.r_pm[ps_]])
                    core(b, score_mm, 1.0 / float(np.sqrt(192.0)), mask_op, hd)
                else:
                    def score_mm(r, lt, sbk, b=b):
                        P.add("pe", lambda e: e.matmul(
                            out=c.ps[sbk][:], lhsT=c.kn[:, r, lt * 128:(lt + 1) * 128],
                            rhs=c.q1[:, b * 512:(b + 1) * 512], start=True, stop=True),
                            r=[c.r_kn, c.r_q1], w=[c.r_ps[sbk]])

                    def mask_op(k, ps_, b=b):
                        ds_ = st["dm"] % 3
                        st["dm"] += 1
                        P.add("pool", lambda e: e.dma_start(out=c.dm[ds_][:], in_=G["dmask"][b, k]),
                              w=[c.r_dm[ds_]], dma=True)
                        P.add("dve", lambda e: e.tensor_tensor(
                            out=c.pm[ps_][:], in0=c.pt[ps_][:], in1=c.dm[ds_][:], op=ALU.mult),
                            r=[c.r_dm[ds_], c.r_pt[ps_]], w=[c.r_pm[ps_]])
                    core(b, score_mm, 1.0 / float(np.sqrt(128.0)), mask_op, NH + hd)


def attn_out_phase(P, c, x_in, x_out, w_o, final=False):
    xin_v = x_in.rearrange("(kc kp) t -> kp kc t", kp=128)
    xout_v = x_out.rearrange("(kc kp) t -> kp kc t", kp=128)
    rot = Rot()
    for i in range(KC):
        ws = rot.w % 2
        rot.w += 1
        P.add("pool", lambda e, ws=ws, i=i: e.dma_start(
            out=c.wp[ws][:], in_=w_o[:, i * 128:(i + 1) * 128].rearrange("(kc kp) f -> kp kc f", kp=128)),
            w=[c.r_wp[ws]], dma=True)
        for b in range(2):
            bk = rot.bank % 8
            rot.bank += 1
            for k in range(KC):
                P.add("pe", lambda e, ws=ws, k=k, b=b, bk=bk: e.matmul(
                    out=c.ps[bk][:], lhsT=c.wp[ws][:, k, :], rhs=c.h[:, k, b * 512:(b + 1) * 512],
                    start=(k == 0), stop=(k == KC - 1)),
                    r=[c.r_wp[ws], c.r_h[k]], w=[c.r_ps[bk]])
            P.add("act", lambda e, i=i, b=b, bk=bk: e.copy(out=c.big[:, i, b * 512:(b + 1) * 512], in_=c.ps[bk][:]),
                  r=[c.r_ps[bk]], w=[c.r_big[i]])
    rms_stats(P, c, lambda i: c.big[:, i, :], c.r_big, KC, D, c.rstd, c.r_rstd)
    gp = c.gb + G_MIXPOST
    for i in range(KC):
        xs = i % 2
        ys = i % 2
        P.add("sp", lambda e, i=i, xs=xs: e.dma_start(out=c.xb[xs][:], in_=xin_v[:, i, :]), w=[c.r_xb[xs]], dma=True)
        P.add("dve", lambda e, i=i, ys=ys: e.scalar_tensor_tensor(
            out=c.yb[ys][:], in0=c.big[:, i, :], scalar=c.gains[:, gp + i:gp + i + 1], in1=c.rstd[:],
            op0=ALU.mult, op1=ALU.mult),
            r=[c.r_big[i], c.r_gains, c.r_rstd], w=[c.r_yb[ys]])
        P.add("dve", lambda e, xs=xs, ys=ys: e.tensor_tensor(
            out=c.yb[ys][:], in0=c.yb[ys][:], in1=c.xb[xs][:], op=ALU.add),
            r=[c.r_yb[ys], c.r_xb[xs]], w=[c.r_yb[ys]])
        P.add("sp", lambda e, i=i, ys=ys: e.dma_start(out=xout_v[:, i, :], in_=c.yb[ys][:]),
              r=[c.r_yb[ys]], dma=True, final=final)


def attn_tables(core):
    tok = tok_index(core)
    qidx = np.ascontiguousarray(np.broadcast_to(tok.astype(np.float32), (128, TOK)))
    kpos = np.stack([tok_index(r) for r in range(NCORE)]).reshape(64, 128)
    kidx = np.ascontiguousarray(kpos.T.astype(np.float32))
    dmask = np.zeros((2, 64, 128, 512), np.float32)
    for b in range(2):
        q = tok[b * 512:(b + 1) * 512]
        for k in range(64):
            dl = q[None, :] - kpos[k][:, None]
            ok = dl >= 0
            dmask[b, k] = ((ok & (dl <= 128)).astype(np.float32) + (ok & (dl % 4 == 0) & (dl <= 512))
                           + (ok & (dl % 16 == 0) & (dl <= 2048)))
    return qidx, kidx, dmask


W_SPECS = {
    "ffn1_w_gate": [DEPTH, D, DFF], "ffn1_w_up": [DEPTH, D, DFF], "ffn1_w_down": [DEPTH, DFF, D],
    "ffn2_w_gate": [DEPTH, D, DFF], "ffn2_w_up": [DEPTH, D, DFF], "ffn2_w_down": [DEPTH, DFF, D],
    "w_in": [DEPTH, D, W_IN_COLS], "w_sw": [DEPTH, D, 576], "mla_w_uq": [DEPTH, 512, 1536],
    "w_uq_sw": [DEPTH, 512, 512], "mla_w_ukv": [DEPTH, 512, 2048], "w_vd": [DEPTH, D, 1024],
    "w_va": [DEPTH, 512, 1024], "w_o": [DEPTH, D, D],
}


def build_fused(depth=DEPTH):
    nc = bass.Bass("TRN2", target_bir_lowering=False)
    x_in = nc.dram_tensor("x_in", [D, TOK], F32, kind="ExternalInput").ap()
    pos = nc.dram_tensor("pos", [128, TOK], I32, kind="ExternalInput").ap()
    gains = nc.dram_tensor("gains", [128, GW], F32, kind="ExternalInput").ap()
    W = {k: nc.dram_tensor(k, shp, F32, kind="ExternalInput").ap() for k, shp in W_SPECS.items()}
    qidx = nc.dram_tensor("qidx", [128, TOK], F32, kind="ExternalInput").ap()
    kidx = nc.dram_tensor("kidx", [128, 64], F32, kind="ExternalInput").ap()
    dmask = nc.dram_tensor("dmask", [2, 64, 128, 512], BF16, kind="ExternalInput").ap()
    x_out = nc.dram_tensor("x_out", [D, TOK], F32, kind="ExternalOutput").ap()
    xs = [nc.dram_tensor("xs%d" % i, [D, TOK], F32, kind="Internal").ap() for i in range(2)]
    qn = nc.dram_tensor("q_n", [NH, 128, TOK], BF16, kind="Internal").ap()
    qr = nc.dram_tensor("q_r", [NH, 64, TOK], BF16, kind="Internal").ap()
    qd = nc.dram_tensor("q_d", [NH, 128, TOK], BF16, kind="Internal").ap()
    KT_s = [nc.dram_tensor("KT_s%d" % i, [KT_ROWS, TOK], BF16, kind="Internal").ap() for i in range(2)]
    V_s = [nc.dram_tensor("V_s%d" % i, [TOK, 2048], BF16, kind="Internal").ap() for i in range(2)]
    KT_g = [nc.dram_tensor("KT_g%d" % i, [NCORE * KT_ROWS, TOK], BF16, kind="Internal").ap() for i in range(2)]
    V_g = [nc.dram_tensor("V_g%d" % i, [NCORE * TOK, 2048], BF16, kind="Internal").ap() for i in range(2)]
    groups = [list(range(NCORE))]
    with ExitStack() as es:
        c = make_ctx(nc, es)
        P = Prog(nc, es)
        load_consts(P, c)
        P.add("sp", lambda e: e.dma_start(out=c.gains[:], in_=gains), w=[c.r_gains], dma=True)
        with ExitStack() as pes:
            c.es = pes
            alloc_rope(c)
            rope_tables(P, c, pos)
            P.barrier()
            P.emit()
        cur = x_in
        nxt = 0

        def ffn(cur, dst, l, which, gb, final=False):
            with ExitStack() as pes:
                c.es = pes
                alloc_ffn(c)
                ffn_phase(P, c, cur, dst, W[which + "_w_gate"][l], W[which + "_w_up"][l], W[which + "_w_down"][l],
                          gb, gb + KC, final=final)
                if final:
                    P.finish()
                P.barrier()
                P.emit()

        for l in range(depth):
            ffn(cur, xs[nxt], l, "ffn1", 64 * (3 * l))
            cur, nxt = xs[nxt], 1 - nxt
            par = l % 2
            outs = {"qn": [qn[h] for h in range(NH)], "qr": [qr[h] for h in range(NH)], "qd": [qd[h] for h in range(NH)],
                    "kn": [KT_s[par][h * 128:(h + 1) * 128, :] for h in range(NH)],
                    "kd": [KT_s[par][1024 + h * 128:1024 + (h + 1) * 128, :] for h in range(NH)],
                    "kr": KT_s[par][2048:2112, :], "va": V_s[par][:, 0:1024], "vd": V_s[par][:, 1024:2048]}
            with ExitStack() as pes:
                c.es = pes
                c.gb = 64 * (3 * l + 1)
                alloc_proj(c)
                proj_phase(P, c, cur, W["w_in"][l], W["w_sw"][l], W["mla_w_uq"][l], W["w_uq_sw"][l],
                           W["mla_w_ukv"][l], W["w_vd"][l], W["w_va"][l], outs)
                P.barrier()
                P.add("pool", lambda e, par=par: e.collective_compute(
                    "AllGather", ALU.bypass, replica_groups=groups, ins=[KT_s[par]], outs=[KT_g[par]]), cc=True)
                P.add("pool", lambda e, par=par: e.collective_compute(
                    "AllGather", ALU.bypass, replica_groups=groups, ins=[V_s[par]], outs=[V_g[par]]), cc=True)
                P.barrier()
                P.emit()
            with ExitStack() as pes:
                c.es = pes
                alloc_attn(c)
                G = {"KT_g": KT_g[par], "V_g": V_g[par], "qn": outs["qn"], "qr": outs["qr"], "qd": outs["qd"],
                     "qidx": qidx, "kidx": kidx, "dmask": dmask}
                attn_phase(P, c, G)
                P.barrier()
                P.emit()
            with ExitStack() as pes:
                c.es = pes
                alloc_attn_out(c)
                attn_out_phase(P, c, cur, xs[nxt], W["w_o"][l])
                P.barrier()
                P.emit()
            cur, nxt = xs[nxt], 1 - nxt
            last = (l == depth - 1)
            ffn(cur, x_out if last else xs[nxt], l, "ffn2", 64 * (3 * l + 2), final=last)
            if not last:
                cur, nxt = xs[nxt], 1 - nxt
    return nc


_PROGS = {}


def kernel(x, positions, ffn1_pre_g, ffn1_post_g, ffn1_w_gate, ffn1_w_up, ffn1_w_down,
           mix_pre_g, mix_post_g, w_in, mla_q_norm_g, mla_w_uq, mla_kv_norm_g, mla_w_ukv,
           w_o, ffn2_pre_g, ffn2_post_g, ffn2_w_gate, ffn2_w_up, ffn2_w_down):
    f32 = lambda a: np.ascontiguousarray(np.asarray(a), dtype=np.float32)
    x = f32(x)
    positions = np.asarray(positions)
    toks = [tok_index(c) for c in range(NCORE)]
    gains = np.zeros((128, GW), np.float32)
    for l in range(DEPTH):
        b = 64 * 3 * l
        gains[:, b:b + KC] = pack_gain(f32(ffn1_pre_g[l]))
        gains[:, b + KC:b + 2 * KC] = pack_gain(f32(ffn1_post_g[l]))
        b += 64
        gains[:, b + G_MIXPRE:b + G_MIXPRE + KC] = pack_gain(f32(mix_pre_g[l]))
        gains[:, b + G_QN:b + G_QN + 4] = f32(mla_q_norm_g[l]).reshape(4, 128).T
        gains[:, b + G_KVN:b + G_KVN + 4] = f32(mla_kv_norm_g[l]).reshape(4, 128).T
        gains[:, b + G_MIXPOST:b + G_MIXPOST + KC] = pack_gain(f32(mix_post_g[l]))
        b += 64
        gains[:, b:b + KC] = pack_gain(f32(ffn2_pre_g[l]))
        gains[:, b + KC:b + 2 * KC] = pack_gain(f32(ffn2_post_g[l]))
    gains[:, G_ROPEC:G_ROPEC + 8] = rope_consts()
    w_in, mla_w_uq, mla_w_ukv = f32(w_in), f32(mla_w_uq), f32(mla_w_ukv)
    pw = [proj_weights(w_in[l], mla_w_uq[l], mla_w_ukv[l]) for l in range(DEPTH)]
    shared = {
        "ffn1_w_gate": f32(ffn1_w_gate), "ffn1_w_up": f32(ffn1_w_up), "ffn1_w_down": f32(ffn1_w_down),
        "ffn2_w_gate": f32(ffn2_w_gate), "ffn2_w_up": f32(ffn2_w_up), "ffn2_w_down": f32(ffn2_w_down),
        "w_in": w_in, "mla_w_uq": mla_w_uq, "mla_w_ukv": mla_w_ukv, "w_o": f32(w_o),
        "w_sw": np.stack([p[0] for p in pw]), "w_uq_sw": np.stack([p[1] for p in pw]),
        "w_vd": np.stack([p[2] for p in pw]), "w_va": np.stack([p[3] for p in pw]),
        "gains": gains,
    }
    if "fused" not in _PROGS:
        _PROGS["fused"] = build_fused()
    nc = _PROGS["fused"]
    in_maps = []
    for c in range(NCORE):
        qidx, kidx, dmask = attn_tables(c)
        m = dict(shared)
        m["x_in"] = np.ascontiguousarray(x[0][toks[c]].T)
        m["pos"] = np.ascontiguousarray(np.broadcast_to(positions[0][toks[c]].astype(np.int32), (128, TOK)))
        m["qidx"], m["kidx"] = qidx, kidx
        m["dmask"] = dmask.astype(mybir.dt.np(BF16))
        in_maps.append(m)
    res = run_bass_kernel_spmd(nc, in_maps, core_ids=list(range(NCORE)))
    out = np.zeros((1, S, D), np.float32)
    for c in range(NCORE):
        out[0, toks[c]] = res.results[c]["x_out"].T
    return out
```
